# Optimizing a Trainium2 kernel written in Bass

```python
import math
import jax, jax.numpy as jnp
from jax import lax
import numpy as np

D_MODEL = 1024
BATCH = 4
SEQ = 8192
DEPTH = 2

A_HEADS = 4
A_QK_DIM = 64
A_V_DIM = 2 * A_QK_DIM
A_QK_COLS = A_HEADS * 2 * A_QK_DIM
A_WIDTH = A_HEADS * A_V_DIM
ROPE_THETA = 500000.0
ROPE_DIM = A_QK_DIM // 4
Q_BLOCK = 128
NEG_INF = -1e30
C_WIDTH = 512
CONV_WIDTH = 3
R_HEADS = 4
R_QK_DIM = 64
R_V_DIM = 2 * R_QK_DIM
R_QK_COLS = R_HEADS * R_QK_DIM
R_WIDTH = R_HEADS * R_V_DIM
R_CHUNK = 128
RET_THETA = 10000.0
N_BRANCH = 3
BRANCH_WIDTH = 512
EPS = 1e-6

SPLIT_SIZES = (A_QK_COLS, A_QK_COLS, A_WIDTH, A_WIDTH,
               C_WIDTH, C_WIDTH, C_WIDTH, C_WIDTH,
               R_QK_COLS, R_QK_COLS, R_WIDTH, R_WIDTH,
               N_BRANCH * D_MODEL)
IN_COLS = sum(SPLIT_SIZES)

kernel_name = "gated_parallel_diffattn_shortconv_retention"


def rms_norm(x, g=None):
    xf = x.astype(jnp.float32)
    y = xf * lax.rsqrt(jnp.mean(xf * xf, axis=-1, keepdims=True) + EPS)
    return y if g is None else y * g.astype(jnp.float32)


def rotate(x, cos, sin):
    x1, x2 = jnp.split(x, 2, axis=-1)
    return jnp.concatenate([x1 * cos - x2 * sin, x2 * cos + x1 * sin], axis=-1)


def diff_attention(q, k, v, lam, subln_g, lam_init):
    b, s = q.shape[0], q.shape[1]
    pos = jnp.arange(s, dtype=jnp.float32)
    inv = ROPE_THETA ** (-jnp.arange(0, ROPE_DIM, 2, dtype=jnp.float32) / ROPE_DIM)
    ang = pos[:, None] * inv[None, :]
    cos = jnp.cos(ang)[:, None, None, :]
    sin = jnp.sin(ang)[:, None, None, :]
    q = jnp.concatenate([rotate(q[..., :ROPE_DIM], cos, sin), q[..., ROPE_DIM:]], axis=-1)
    k = jnp.concatenate([rotate(k[..., :ROPE_DIM], cos, sin), k[..., ROPE_DIM:]], axis=-1)
    q = q * (A_QK_DIM ** -0.5)
    kh = k.transpose(0, 2, 3, 1, 4)
    vh = v.transpose(0, 2, 1, 3)
    nb = s // Q_BLOCK
    qb = q.transpose(0, 2, 3, 1, 4).reshape(b, A_HEADS, 2, nb, Q_BLOCK, A_QK_DIM)
    qb = qb.transpose(3, 0, 1, 2, 4, 5)
    kpos = jnp.arange(s)

    def block(args):
        qi, start = args
        sc = jnp.einsum('bhcqd,bhckd->bhcqk', qi, kh).astype(jnp.float32)
        qpos = start + jnp.arange(Q_BLOCK)
        mask = kpos[None, :] <= qpos[:, None]
        p = jax.nn.softmax(jnp.where(mask, sc, NEG_INF), axis=-1)
        w = p[:, :, 0] - lam * p[:, :, 1]
        return jnp.einsum('bhqk,bhkv->bhqv', w, vh)

    out = lax.map(block, (qb, jnp.arange(nb, dtype=jnp.int32) * Q_BLOCK))
    out = out.transpose(1, 0, 3, 2, 4).reshape(b, s, A_HEADS, A_V_DIM)
    out = rms_norm(out, subln_g) * (1.0 - lam_init)
    return out.reshape(b, s, A_WIDTH)


def short_conv(x_in, gate_b, gate_c, w):
    u = gate_c * x_in
    y = lax.conv_general_dilated(u, w[:, None, :].astype(u.dtype), window_strides=(1,),
                                 padding=[(CONV_WIDTH - 1, 0)],
                                 dimension_numbers=('NWC', 'WIO', 'NWC'),
                                 feature_group_count=C_WIDTH)
    return gate_b * y


def retention(q, k, v):
    b, s = q.shape[0], q.shape[1]
    pos = jnp.arange(s, dtype=jnp.float32)
    inv = 1.0 / (RET_THETA ** jnp.linspace(0.0, 1.0, R_QK_DIM // 2, dtype=jnp.float32))
    ang = pos[:, None] * inv[None, :]
    cos = jnp.cos(ang)[:, None, :]
    sin = jnp.sin(ang)[:, None, :]
    q = rotate(q, cos, sin)
    k = rotate(k, cos, sin) * (R_QK_DIM ** -0.5)
    log_g = jnp.log(1.0 - 2.0 ** (-5.0 - jnp.arange(R_HEADS, dtype=jnp.float32)))
    nc = s // R_CHUNK

    def chunks(t):
        return t.reshape(b, nc, R_CHUNK, R_HEADS, t.shape[-1]).transpose(0, 3, 1, 2, 4)

    qc, kc, vc = chunks(q), chunks(k), chunks(v)
    idx = jnp.arange(R_CHUNK, dtype=jnp.float32)
    diff = idx[:, None] - idx[None, :]
    dmask = jnp.where(diff >= 0,
                      jnp.exp(jnp.where(diff >= 0, diff, 0.0)[None] * log_g[:, None, None]),
                      0.0)
    inner = jnp.einsum('bhncd,bhnmd->bhncm', qc, kc) * dmask[None, :, None]
    inner = jnp.einsum('bhncm,bhnme->bhnce', inner, vc)
    zeta = jnp.exp((R_CHUNK - 1 - idx)[None, :] * log_g[:, None])
    kv = jnp.einsum('bhnmd,bhnme->bhnde', kc * zeta[None, :, None, :, None], vc)
    chunk_decay = jnp.exp(R_CHUNK * log_g)[None, :, None, None]

    def step(state, kv_n):
        return (chunk_decay * state + kv_n).astype(kv_n.dtype), state

    init = jnp.zeros((b, R_HEADS, R_QK_DIM, R_V_DIM), kv.dtype)
    _, prev = lax.scan(step, init, kv.transpose(2, 0, 1, 3, 4))
    prev = prev.transpose(1, 2, 0, 3, 4)
    xi = jnp.exp((idx + 1.0)[None, :] * log_g[:, None])
    cross = jnp.einsum('bhncd,bhnde->bhnce', qc, prev) * xi[None, :, None, :, None]
    out = (inner + cross).transpose(0, 2, 3, 1, 4).reshape(b, s, R_HEADS, R_V_DIM)
    out = rms_norm(out)
    return out.reshape(b, s, R_WIDTH)


def hybrid_layer(x, norm_g, w_in, attn_lambda, attn_subln_g, conv_w, w_branch, w_out, layer):
    b, s, _ = x.shape
    h = rms_norm(x, norm_g)
    proj = h @ w_in
    split_points = [int(p) for p in np.cumsum(SPLIT_SIZES)[:-1]]
    (aq, ak, av, az, cx, cb, cc, cz, rq, rk, rv, rz, gates) = jnp.split(proj, split_points, axis=-1)

    lam_init = 0.8 - 0.6 * math.exp(-0.3 * layer)
    lp = attn_lambda.astype(jnp.float32)
    lam = jnp.exp(jnp.sum(lp[0] * lp[1])) - jnp.exp(jnp.sum(lp[2] * lp[3])) + lam_init
    a = diff_attention(aq.reshape(b, s, A_HEADS, 2, A_QK_DIM),
                       ak.reshape(b, s, A_HEADS, 2, A_QK_DIM),
                       av.reshape(b, s, A_HEADS, A_V_DIM),
                       lam, attn_subln_g, lam_init) * jax.nn.silu(az)
    c = short_conv(cx, cb, cc, conv_w) * jax.nn.silu(cz)
    r = retention(rq.reshape(b, s, R_HEADS, R_QK_DIM),
                  rk.reshape(b, s, R_HEADS, R_QK_DIM),
                  rv.reshape(b, s, R_HEADS, R_V_DIM)) * jax.nn.silu(rz)

    g = jax.nn.sigmoid(gates.astype(jnp.float32)).reshape(b, s, N_BRANCH, D_MODEL)
    branches = (a, c, r)
    merged = g[:, :, 0] * (branches[0] @ w_branch[0])
    for i in range(1, N_BRANCH):
        merged = merged + g[:, :, i] * (branches[i] @ w_branch[i])
    return x + (merged @ w_out).astype(x.dtype)


def setup_inputs(seed: int = 0) -> dict:
    key = jax.random.key(seed)
    ks = jax.random.split(key, 10)
    f32 = jnp.float32
    x = jax.random.normal(ks[0], (BATCH, SEQ, D_MODEL), f32)
    norm_g = 1.0 + 0.01 * jax.random.normal(ks[1], (DEPTH, D_MODEL), f32)
    w_in = jax.random.normal(ks[2], (DEPTH, D_MODEL, IN_COLS), f32) * D_MODEL ** -0.5
    attn_lambda = 0.1 * jax.random.normal(ks[3], (DEPTH, 4, A_QK_DIM), f32)
    attn_subln_g = 1.0 + 0.01 * jax.random.normal(ks[4], (DEPTH, A_V_DIM), f32)
    conv_w = jax.random.normal(ks[5], (DEPTH, CONV_WIDTH, C_WIDTH), f32) * CONV_WIDTH ** -0.5
    w_branch = jax.random.normal(ks[6], (DEPTH, N_BRANCH, BRANCH_WIDTH, D_MODEL), f32) * BRANCH_WIDTH ** -0.5
    w_out = jax.random.normal(ks[7], (DEPTH, D_MODEL, D_MODEL), f32) * D_MODEL ** -0.5
    final_norm_g = 1.0 + 0.01 * jax.random.normal(ks[8], (D_MODEL,), f32)
    return {"x": x, "norm_g": norm_g, "w_in": w_in, "attn_lambda": attn_lambda,
            "attn_subln_g": attn_subln_g, "conv_w": conv_w, "w_branch": w_branch,
            "w_out": w_out, "final_norm_g": final_norm_g}


def reference(x, norm_g, w_in, attn_lambda, attn_subln_g, conv_w, w_branch, w_out, final_norm_g):
    for layer in range(DEPTH):
        x = hybrid_layer(x, norm_g[layer], w_in[layer], attn_lambda[layer], attn_subln_g[layer],
                         conv_w[layer], w_branch[layer], w_out[layer], layer)
    return rms_norm(x, final_norm_g).astype(x.dtype)
```

```python
import math
import os
STOP = int(os.environ.get('KSTOP', '99'))
STQ = os.environ.get('KSTQ', 'pool')
SUB = os.environ.get('KSUB', 'abcdefghi')
DBG = int(os.environ.get('KDBG', '0'))
from contextlib import ExitStack
import numpy as np
import concourse.bass as bass
import concourse.mybir as mybir
from concourse.bass_utils import run_bass_kernel_spmd

F32 = mybir.dt.float32
BF16 = mybir.dt.bfloat16
ALU = mybir.AluOpType
ACTF = mybir.ActivationFunctionType

D = 1024
NCH = 8
T = 512
EPS = 1e-6
ENGINES = ("pe", "act", "dve", "pool", "sp")
EPOCH = 30000


class Res:
    __slots__ = ("name", "writer", "readers", "last_dma")

    def __init__(self, name):
        self.name = name
        self.writer = None
        self.readers = []
        self.last_dma = None


class Op:
    __slots__ = ("eng", "fn", "deps", "milestone", "sem", "val", "is_dma", "dma_res", "idx")

    def __init__(self, eng, fn):
        self.eng = eng
        self.fn = fn
        self.deps = []
        self.milestone = False
        self.sem = None
        self.val = 0
        self.is_dma = False
        self.dma_res = None
        self.idx = 0


class Prog:
    def __init__(self, nc):
        self.nc = nc
        self.ops = {e: [] for e in ENGINES}
        self.n = 0
        self.dma_res = []

    def _add(self, op, reads, writes):
        deps = set()
        for r in reads:
            if r.writer is not None:
                deps.add(r.writer)
        for w in writes:
            if w.writer is not None:
                deps.add(w.writer)
            for rd in w.readers:
                deps.add(rd)
        deps.discard(op)
        best = {}
        keep = []
        for d in deps:
            if d.is_dma:
                keep.append(d)
                continue
            if d.eng == op.eng and not op.is_dma:
                if op.eng == "pe":
                    continue
            if d.eng not in best or best[d.eng].idx < d.idx:
                best[d.eng] = d
        keep.extend(best.values())
        for d in keep:
            d.milestone = True
        op.deps = keep
        for r in reads:
            if op.is_dma:
                r.readers.append(op)
            else:
                r.readers = [x for x in r.readers if x.is_dma or x.eng != op.eng]
                r.readers.append(op)
        for w in writes:
            w.writer = op
            w.readers = []
        op.idx = self.n
        self.n += 1
        self.ops[op.eng].append(op)
        return op

    skip = False

    def op(self, eng, fn, reads=(), writes=()):
        if self.skip:
            return None
        return self._add(Op(eng, fn), list(reads), list(writes))

    def dma(self, fn, sb, reads=(), writes=(), eng="sp"):
        if self.skip:
            return None
        op = Op(eng, fn)
        op.is_dma = True
        op.dma_res = (sb, eng)
        op.milestone = True
        o = self._add(op, list(reads), list(writes))
        if sb.last_dma is not None and sb.last_dma not in o.deps:
            o.deps.append(sb.last_dma)
        sb.last_dma = o
        if (sb, eng) not in self.dma_res:
            self.dma_res.append((sb, eng))
        return o

    def emit(self, stack, final_waits=()):
        nc = self.nc
        for e in ENGINES:
            cnt = 0
            sems = []
            for op in self.ops[e]:
                if op.is_dma or not op.milestone:
                    continue
                ep = cnt // EPOCH
                while len(sems) <= ep:
                    sems.append(stack.enter_context(nc.semaphore(f"s_{e}_{len(sems)}")))
                op.sem = sems[ep]
                op.val = cnt % EPOCH + 1
                cnt += 1
        dsem = {}
        dcnt = {}
        for (r, en) in self.dma_res:
            dsem[(id(r), en)] = stack.enter_context(nc.semaphore(f"d_{r.name}_{en}"))
            dcnt[(id(r), en)] = 0
        allops = sorted([o for e in ENGINES for o in self.ops[e]], key=lambda o: o.idx)
        for op in allops:
            if op.is_dma:
                k = (id(op.dma_res[0]), op.dma_res[1])
                dcnt[k] += 16
                if dcnt[k] > 16 * 3000:
                    raise RuntimeError("dma sem overflow risk " + op.dma_res[0].name)
                op.sem = dsem[k]
                op.val = dcnt[k]
        self.stats = {e: len(self.ops[e]) for e in ENGINES}
        nwaits = {e: 0 for e in ENGINES}

        def run(e, eng):
            waited = {}
            for op in self.ops[e]:
                need = {}
                for d in op.deps:
                    k = id(d.sem)
                    if waited.get(k, 0) >= d.val:
                        continue
                    if k not in need or need[k][1] < d.val:
                        need[k] = (d.sem, d.val)
                for k, (s, v) in need.items():
                    eng.wait_ge(s, v)
                    waited[k] = v
                    nwaits[e] += 1
                ins = op.fn(eng)
                if op.milestone:
                    ins.then_inc(op.sem, 16 if op.is_dma else 1)
            if e == "sp":
                for d in final_waits:
                    eng.wait_ge(d.sem, d.val)

        block = stack.enter_context(nc.Block())

        @block.tensor
        def _(eng):
            run("pe", eng)

        @block.scalar
        def _(eng):
            run("act", eng)

        @block.vector
        def _(eng):
            run("dve", eng)

        @block.gpsimd
        def _(eng):
            run("pool", eng)

        @block.sync
        def _(eng):
            run("sp", eng)

        self.nwaits = nwaits


def _const_tables(S):
    pos = np.arange(S, dtype=np.float64)
    inv = 500000.0 ** (-np.arange(0, 16, 2, dtype=np.float64) / 16.0)
    ang = (pos.astype(np.float32)[:, None] * inv.astype(np.float32)[None, :]).astype(np.float32).astype(np.float64)
    ca, sa = np.cos(ang), np.sin(ang)
    ca8 = np.tile(ca, (1, 8))
    sa8 = np.tile(sa, (1, 8))
    invr = 1.0 / (10000.0 ** np.linspace(0.0, 1.0, 32, dtype=np.float32).astype(np.float64))
    angr = (pos.astype(np.float32)[:, None] * invr.astype(np.float32)[None, :]).astype(np.float32).astype(np.float64)
    cr, sr = np.cos(angr), np.sin(angr)
    crq = np.tile(cr, (1, 4)); srq = np.tile(sr, (1, 4))
    crk = crq * 0.125; srk = srq * 0.125
    rope = np.concatenate([ca8 * 0.125, sa8 * 0.125, ca8, sa8,
                           crq, crk, srq, srk], axis=1)
    rope = rope.astype(np.float32)
    log_g = np.log(1.0 - 2.0 ** (-5.0 - np.arange(4, dtype=np.float64)))
    idx = np.arange(128, dtype=np.float64)
    diff = idx[:, None] - idx[None, :]
    dmask = np.where(diff >= 0, np.exp(np.where(diff >= 0, diff, 0.0)[None] * log_g[:, None, None]), 0.0)
    dmT = np.ascontiguousarray(dmask.transpose(2, 0, 1)).astype(np.float32)
    zeta = np.exp((127 - idx)[None, :] * log_g[:, None])
    xi = np.exp((idx + 1.0)[None, :] * log_g[:, None])
    zetaK = np.repeat(zeta.T[:, :, None], 64, axis=2).reshape(128, 256).astype(np.float32)
    xiQ = np.repeat(xi.T[:, :, None], 64, axis=2).reshape(128, 256).astype(np.float32)
    decay = [float(np.exp(128 * lg)) for lg in log_g]
    m01 = (idx[None, :] >= idx[:, None]).astype(np.float32)
    mask01 = np.ascontiguousarray(np.stack([m01, m01], axis=1))
    ident = np.eye(128, dtype=np.float32)
    NT_ = S // T
    ropeA = np.ascontiguousarray(rope[:NT_ * T, 0:256].reshape(NT_, 4, 128, 4, 64).transpose(0, 2, 3, 1, 4)).reshape(NT_, 128, 1024)
    return dict(rope=rope, ropeA=ropeA, dmT=dmT.reshape(128, 512), zetaK=zetaK, xiQ=xiQ, mask01=mask01.reshape(128, 256),
                ident=ident), decay


def _layout_weights(w_in, w_branch, w_out):
    wi = w_in.reshape(NCH, 128, -1)
    win = wi[:, :, :5632].reshape(NCH, 128, 11, 512).transpose(2, 1, 0, 3)
    win = np.ascontiguousarray(win).reshape(11, 128, NCH * 512)
    gates = wi[:, :, 5632:].reshape(NCH, 128, 3, 8, 128).transpose(3, 1, 0, 2, 4)
    gates = gates.reshape(8, 128, NCH * 3 * 128)
    wb = w_branch.reshape(3, 4, 128, 8, 128).transpose(3, 2, 0, 1, 4)
    wb = wb.reshape(8, 128, 3 * 4 * 128)
    wpost = np.ascontiguousarray(np.concatenate([gates, wb], axis=2))
    wo = np.ascontiguousarray(w_out.reshape(NCH, 128, 8, 128).transpose(2, 1, 0, 3)).reshape(8, 128, NCH * 128)
    return win, wpost, wo


def build(S, NL):
    NT = S // T
    nc = bass.Bass("TRN2", target_bir_lowering=False)
    dt = nc.dram_tensor

    def din(name, shape, dtype=F32):
        return dt(name, list(shape), dtype, kind="ExternalInput").ap()

    xT_d = din("xT", [D, S])
    rope_d = din("rope", [S, 768])
    ropeA_d = din("ropeA", [NT, 128, 1024])
    dmT_d = din("dmT", [128, 512])
    zetaK_d = din("zetaK", [128, 256])
    xiQ_d = din("xiQ", [128, 256])
    mask01_d = din("mask01", [128, 256])
    ident_d = din("ident", [128, 128])
    fng_d = din("fng", [128, NCH])
    win_d = [din(f"win{l}", [11, 128, 4096]) for l in range(NL)]
    wpost_d = [din(f"wpost{l}", [8, 128, 4608]) for l in range(NL)]
    wout_d = [din(f"wout{l}", [8, 128, 1024]) for l in range(NL)]
    ng_d = [din(f"ng{l}", [128, NCH]) for l in range(NL)]
    sg_d = [din(f"sg{l}", [128, 1]) for l in range(NL)]
    cw_d = [din(f"cw{l}", [128, 12]) for l in range(NL)]
    al_d = [din(f"al{l}", [128, 256]) for l in range(NL)]
    outT_d = dt("outT", [D, S], F32, kind="ExternalOutput").ap()
    if DBG:
        dbg_d = dt("dbg", [128, 5, 2048], F32, kind="ExternalOutput").ap()
    winb_d = [dt(f"winb{l}", [11, 128, 4096], BF16, kind="Internal").ap() for l in range(NL)]
    wpostb_d = [dt(f"wpostb{l}", [8, 128, 4608], BF16, kind="Internal").ap() for l in range(NL)]
    woutb_d = [dt(f"woutb{l}", [8, 128, 1024], BF16, kind="Internal").ap() for l in range(NL)]
    x1T_d = dt("x1T", [D, S], F32, kind="Internal").ap()
    ktc_d = [dt(f"ktc{l}", [128, 4, 2, S], BF16, kind="Internal").ap() for l in range(NL)]
    vc_d = [dt(f"vc{l}", [S, 512], BF16, kind="Internal").ap() for l in range(NL)]

    _, DECAY = _const_tables(128)

    st = ExitStack()
    with st:
        P = Prog(nc)

        def sb(name, shape, dtype):
            return st.enter_context(nc.sbuf_tensor(name, list(shape), dtype)), Res(name)

        banks = []
        for i in range(8):
            banks.append((st.enter_context(nc.psum_tensor(f"bank{i}", [128, 512], F32)), Res(f"bank{i}")))
        bank_rr = [0]

        def nb():
            b = banks[bank_rr[0] % 8]
            bank_rr[0] += 1
            return b

        xt, r_xt = sb("xt", [128, NCH, T], F32)
        ht, r_ht = sb("ht", [128, NCH, T], BF16)
        sq = [sb(f"sq{i}", [128, T], F32) for i in range(2)]
        rstd, r_rstd = sb("rstd", [128, T], F32)
        wbuf = [sb(f"wbuf{i}", [128, NCH, 512], BF16) for i in range(2)]
        qt, r_qt = sb("qt", [128, 4, T], BF16)
        ktsb, r_ktsb = sb("ktsb", [128, 4, 2, T], BF16)
        q32, r_q32 = sb("q32", [128, 4, 512], F32)
        qk32, r_qk32 = q32[:, 0, :], r_q32
        rqk, r_rqk = q32[:, 1, :], r_q32
        ropeA, r_ropeA = sb("ropeA_s", [128, 4, 4, 64], F32)
        rt1, r_rt1 = sb("rt1", [128, 256], F32)
        rt2, r_rt2 = sb("rt2", [128, 256], F32)
        ropet = [sb(f"ropet{i}", [128, 512], F32) for i in range(2)]
        za, r_za = sb("za", [128, 4, T], F32)
        fA, r_fA = sb("fA", [128, 4, T], F32)
        uext, r_uext = sb("uext", [128, 4, T + 2], F32)
        fB, r_fB = sb("fB", [128, 4, T], F32)
        ct, r_ct = sb("ct", [128, 4, T], BF16)
        rT, r_rT = sb("rT", [128, 4, T], BF16)
        aT, r_aT = sb("aT", [128, 4, T], BF16)
        rb4, r_rb4 = sb("rb4", [128, 4, 256], BF16)
        rtr, r_rtr = sb("rtr", [64, 3, 512], BF16)
        vr, r_vr = sb("vr", [128, 4, 512], BF16)
        smT, r_smT = sb("smT", [128, 512], BF16)
        state, r_state = sb("state", [64, 512], F32)
        stateb, r_stateb = sb("stateb", [64, 512], BF16)
        ktb = [sb(f"ktb{i}", [128, T], BF16) for i in range(3)]
        vtb = [sb(f"vtb{i}", [128, 4, 128], BF16) for i in range(3)]
        ptb = [sb(f"ptb{i}", [128, T], BF16) for i in range(3)]
        sx, _ = sb("sx", [128, 4, T], F32)
        s1, r_s1 = sx[:, 0, :], Res("s1")
        s2, r_s2 = sx[:, 1, :], Res("s2")
        s3, r_s3 = sx[:, 2, :], Res("s3")
        s4, r_s4 = sx[:, 3, :], Res("s4")
        r_sx = [r_s1, r_s2, r_s3, r_s4]
        wpb = [sb(f"wpb{i}", [128, 4608], BF16) for i in range(2)]
        mg, r_mg = sb("mg", [128, NCH, T], BF16)
        qkb4 = [(mg[:, 0:4, :], r_mg), (mg[:, 4:8, :], r_mg)]
        gtb = [(s1, r_s1), (s2, r_s2), (s3, r_s3)]
        vsb, r_vsb = aT, r_aT
        xt2, r_xt2 = sb("xt2", [128, NCH, T], F32)
        xts = [(xt, r_xt), (xt2, r_xt2)]
        wob = [sb(f"wob{i}", [128, NCH, 128], BF16) for i in range(2)]
        dmT, r_dmT = sb("dmT_s", [128, 512], F32)
        zetaK, r_zetaK = sb("zetaK_s", [128, 256], F32)
        xiQ, r_xiQ = sb("xiQ_s", [128, 256], F32)
        mask01, r_mask01 = sb("mask01_s", [128, 256], F32)
        identf, r_identf = sb("identf", [128, 128], F32)
        identb, r_identb = sb("identb", [128, 128], BF16)
        onesf, r_onesf = sb("onesf", [128, 128], F32)
        onesb, r_onesb = sb("onesb", [128, 128], BF16)
        ng = [sb(f"ng_s{l}", [128, NCH], F32) for l in range(NL)]
        fng, r_fng = sb("fng_s", [128, NCH], F32)
        sg = [sb(f"sg_s{l}", [128, 1], F32) for l in range(NL)]
        cw = [sb(f"cw_s{l}", [128, 12], F32) for l in range(NL)]
        al = [sb(f"al_s{l}", [128, 256], F32) for l in range(NL)]
        lamt = [sb(f"lam_s{l}", [128, 8], F32) for l in range(NL)]

        build.sbuf_free = nc.sbuf_bytes_remaining

        def ld(tile_res, src):
            t, r = tile_res
            return P.dma(lambda e, t=t, src=src: e.dma_start(out=t[:], in_=src), r, writes=[r])

        ld((dmT, r_dmT), dmT_d[:, :])
        ld((zetaK, r_zetaK), zetaK_d[:, :])
        ld((xiQ, r_xiQ), xiQ_d[:, :])
        ld((mask01, r_mask01), mask01_d[:, :])
        ld((identf, r_identf), ident_d[:, :])
        ld((fng, r_fng), fng_d[:, :])
        for l in range(NL):
            ld(ng[l], ng_d[l][:, :])
            ld(sg[l], sg_d[l][:, :])
            ld(cw[l], cw_d[l][:, :])
            ld(al[l], al_d[l][:, :])
        P.op("dve", lambda e: e.tensor_copy(out=identb[:], in_=identf[:]), [r_identf], [r_identb])
        P.op("pool", lambda e: e.memset(onesf[:], 1.0), [], [r_onesf])
        P.op("pool", lambda e: e.memset(onesb[:], 1.0), [], [r_onesb])
        P.op("pool", lambda e: e.memset(ktsb[:], 0.0), [], [r_ktsb])
        P.op("pool", lambda e: e.memset(uext[:], 0.0), [], [r_uext])

        for l in range(NL):
            lam_init = 0.8 - 0.6 * math.exp(-0.3 * l)
            a_t, a_r = al[l]
            lt, lr = lamt[l]
            P.op("pool", lambda e, lt=lt: e.memset(lt[:], 0.0), [], [lr])
            P.op("dve", lambda e, a_t=a_t: e.tensor_tensor(out=rt1[:, 0:64], in0=a_t[:, 0:64], in1=a_t[:, 64:128], op=ALU.mult), [a_r], [r_rt1])
            P.op("dve", lambda e, a_t=a_t: e.tensor_tensor(out=rt1[:, 64:128], in0=a_t[:, 128:192], in1=a_t[:, 192:256], op=ALU.mult), [a_r, r_rt1], [r_rt1])
            P.op("act", lambda e, lt=lt: e.activation(out=rt2[:, 0:64], in_=rt1[:, 0:64], func=ACTF.Copy, accum_out=lt[:, 1:2]), [r_rt1], [r_rt2, lr])
            P.op("act", lambda e, lt=lt: e.activation(out=rt2[:, 64:128], in_=rt1[:, 64:128], func=ACTF.Copy, accum_out=lt[:, 2:3]), [r_rt1, r_rt2, lr], [r_rt2, lr])
            P.op("act", lambda e, lt=lt: e.activation(out=lt[:, 3:5], in_=lt[:, 1:3], func=ACTF.Exp), [lr], [lr])
            P.op("dve", lambda e, lt=lt, li=lam_init: e.scalar_tensor_tensor(out=lt[:, 0:1], in0=lt[:, 4:5], scalar=-li, in1=lt[:, 3:4], op0=ALU.add, op1=ALU.subtract), [lr], [lr])

        xt_flat = xt[:].rearrange("p c t -> p (c t)")
        ht_flat = ht[:].rearrange("p c t -> p (c t)")
        wres = {}
        cast_i = [0]

        xt2_flat = xt2[:].rearrange("p c t -> p (c t)")
        mg_flat = mg[:].rearrange("p c t -> p (c t)")
        pipes = [(xt_flat, r_xt, ht_flat, r_ht, "dve"), (xt2_flat, r_xt2, mg_flat, r_mg, "act")]

        def cast_block(src, dst, n, key):
            f_ap, f_r, b_ap, b_r, eng = pipes[cast_i[0] % 2]
            cast_i[0] += 1
            P.dma(lambda e: e.dma_start(out=f_ap[:, 0:n], in_=src), f_r, writes=[f_r])
            if eng == "act":
                P.op("act", lambda e: e.activation(out=b_ap[:, 0:n], in_=f_ap[:, 0:n], func=ACTF.Copy), [f_r], [b_r])
            else:
                P.op(eng, lambda e: e.tensor_copy(out=b_ap[:, 0:n], in_=f_ap[:, 0:n]), [f_r], [b_r])
            r = wres.setdefault(key, Res("w_" + key))
            P.dma(lambda e: e.dma_start(out=dst, in_=b_ap[:, 0:n]), b_r, reads=[b_r], writes=[r], eng=STQ)

        for l in range(NL):
            for g in range(11):
                cast_block(win_d[l][g], winb_d[l][g], 4096, f"in{l}_{g}")
            for j in range(8):
                cast_block(wpost_d[l][j][:, 0:3072], wpostb_d[l][j][:, 0:3072], 3072, f"post{l}_{j}")
                cast_block(wpost_d[l][j][:, 3072:4608], wpostb_d[l][j][:, 3072:4608], 1536, f"post{l}_{j}")
            for jo in range(8):
                cast_block(wout_d[l][jo], woutb_d[l][jo], 1024, f"out{l}_{jo}")

        r_x1 = [Res(f"x1_{t}") for t in range(NT)]
        r_kc = [[Res(f"kc{l}_{t}") for t in range(NT)] for l in range(NL)]
        r_vc = [[Res(f"vc{l}_{t}") for t in range(NT)] for l in range(NL)]
        final = []

        wb_i = [0]

        def rms_scale(src_bank, scale, out_tile, out_res, bres):
            P.op("act", lambda e: e.activation(out=out_tile[:], in_=src_bank[:], func=ACTF.Ln, bias=EPS, scale=scale), [bres], [out_res])
            P.op("act", lambda e: e.activation(out=out_tile[:], in_=out_tile[:], func=ACTF.Exp, scale=-0.5), [out_res], [out_res])

        pool_all = list(range(8))
        pool_B = [4, 5, 6, 7]
        bank_pool = [pool_all]

        def nb():
            lst = bank_pool[0]
            b = banks[lst[bank_rr[0] % len(lst)]]
            bank_rr[0] += 1
            return b

        def interleave(ga, gb, na, nb_):
            ia = ib = 0
            a_done = b_done = False
            while not (a_done and b_done):
                take_a = (not a_done) and (b_done or ia * max(nb_, 1) <= ib * max(na, 1))
                if take_a:
                    try:
                        next(ga)
                        ia += 1
                    except StopIteration:
                        a_done = True
                else:
                    try:
                        next(gb)
                        ib += 1
                    except StopIteration:
                        b_done = True

        norm_done = set()

        def emit_norm(l, t):
            g_idx = l * NT + t
            norm_done.add(g_idx)
            xc, r_xc = xts[g_idx % 2]
            src_d = xT_d if l == 0 else x1T_d
            ng_t, ng_r = ng[l]
            tsl = slice(t * T, (t + 1) * T)
            rd = [r_x1[t]] if l > 0 else []
            P.dma(lambda e: e.dma_start(out=xc[:], in_=src_d.rearrange("(c p) t -> p c t", p=128)[:, :, tsl]), r_xc, reads=rd, writes=[r_xc])
            bk, br = nb()
            for c in range(NCH):
                sq_t, sq_r = sq[c % 2]
                P.op("pool", lambda e, sq_t=sq_t, c=c: e.tensor_tensor(out=sq_t[:], in0=xc[:, c, :], in1=xc[:, c, :], op=ALU.mult), [r_xc], [sq_r])
                P.op("pe", lambda e, sq_t=sq_t, c=c: e.matmul(bk[:], lhsT=onesf[:], rhs=sq_t[:], start=(c == 0), stop=(c == NCH - 1)), [r_onesf, sq_r], [br])
            rms_scale(bk, 1.0 / D, rstd, r_rstd, br)
            for c in range(NCH):
                P.op("dve", lambda e, c=c: e.scalar_tensor_tensor(out=ht[:, c, :], in0=xc[:, c, :], scalar=ng_t[:, c:c + 1], in1=rstd[:],
                                                                   op0=ALU.mult, op1=ALU.mult), [r_xc, ng_r, r_rstd], [r_ht])

        for l in range(NL):
            lam_init = 0.8 - 0.6 * math.exp(-0.3 * l)
            src_d = xT_d if l == 0 else x1T_d
            last = (l == NL - 1)
            ng_t, ng_r = ng[l]
            sg_t, sg_r = sg[l]
            cw_t, cw_r = cw[l]
            lt, lr = lamt[l]
            P.op("pool", lambda e: e.memset(state[:], 0.0), [], [r_state])
            P.op("pool", lambda e: e.memset(stateb[:], 0.0), [], [r_stateb])
            P.op("pool", lambda e: e.memset(uext[:], 0.0), [], [r_uext])

            def load_w(g, l=l):
                w_t, w_r = wbuf[wb_i[0] % 2]
                wb_i[0] += 1
                P.dma(lambda e, w_t=w_t, g=g: e.dma_start(out=w_t[:].rearrange("p c n -> p (c n)"), in_=winb_d[l][g]), w_r,
                      reads=[wres[f"in{l}_{g}"]], writes=[w_r])
                return w_t, w_r

            def proj_tok(w_t, w_r, s):
                bk, br = nb()
                for c in range(NCH):
                    P.op("pe", lambda e, bk=bk, c=c, s=s, w_t=w_t: e.matmul(bk[:], lhsT=ht[:, c, s * 128:(s + 1) * 128], rhs=w_t[:, c, :],
                                                                            start=(c == 0), stop=(c == NCH - 1)), [r_ht, w_r], [br])
                return bk, br

            def proj_feat(w_t, w_r, jj):
                bk, br = nb()
                for c in range(NCH):
                    P.op("pe", lambda e, bk=bk, c=c, jj=jj, w_t=w_t: e.matmul(bk[:], lhsT=w_t[:, c, jj * 128:(jj + 1) * 128], rhs=ht[:, c, :],
                                                                              start=(c == 0), stop=(c == NCH - 1)), [r_ht, w_r], [br])
                return bk, br

            def rope_attn(bk, br, rp_t, rp_r, co, so, dst_t, dst_r, scale):
                v = qk32[:].rearrange("p (g d) -> p g d", d=64)
                dv = dst_t[:].rearrange("p (g d) -> p g d", d=64)
                cs = rp_t[:, co:co + 64].rearrange("p (g f) -> p g f", f=8)
                sn = rp_t[:, so:so + 64].rearrange("p (g f) -> p g f", f=8)
                a1 = rt1[:, 0:64].rearrange("p (g f) -> p g f", f=8)
                a2 = rt2[:, 0:64].rearrange("p (g f) -> p g f", f=8)
                P.op("act", lambda e: e.activation(out=qk32[:], in_=bk[:], func=ACTF.Copy), [br], [r_qk32])
                P.op("pool", lambda e: e.tensor_scalar(out=dst_t[:], in0=qk32[:], scalar1=scale, scalar2=None, op0=ALU.mult), [r_qk32], [dst_r])
                P.op("dve", lambda e: e.tensor_tensor(out=a1, in0=v[:, :, 0:8], in1=cs, op=ALU.mult), [r_qk32, rp_r], [r_rt1])
                P.op("dve", lambda e: e.tensor_tensor(out=a2, in0=v[:, :, 8:16], in1=sn, op=ALU.mult), [r_qk32, rp_r], [r_rt2])
                P.op("dve", lambda e: e.tensor_tensor(out=dv[:, :, 0:8], in0=a1, in1=a2, op=ALU.subtract), [r_rt1, r_rt2, dst_r], [dst_r])
                P.op("dve", lambda e: e.tensor_tensor(out=a1, in0=v[:, :, 8:16], in1=cs, op=ALU.mult), [r_qk32, rp_r, r_rt1], [r_rt1])
                P.op("dve", lambda e: e.tensor_tensor(out=a2, in0=v[:, :, 0:8], in1=sn, op=ALU.mult), [r_qk32, rp_r, r_rt2], [r_rt2])
                P.op("dve", lambda e: e.tensor_tensor(out=dv[:, :, 8:16], in0=a1, in1=a2, op=ALU.add), [r_rt1, r_rt2, dst_r], [dst_r])

            def transpose4(src_t, src_r, dst_t, dst_r, s):
                bk, br = nb()
                for h in range(4):
                    P.op("pe", lambda e, bk=bk, h=h: e.matmul(bk[:, h * 128:(h + 1) * 128], lhsT=src_t[:, h * 128:(h + 1) * 128], rhs=identb[:],
                                                              start=True, stop=True), [src_r, r_identb], [br])
                P.op("act", lambda e, bk=bk: e.activation(out=dst_t[:, :, s * 128:(s + 1) * 128], in_=bk[:].rearrange("p (h n) -> p h n", n=128), func=ACTF.Copy),
                     [br], [dst_r])

            for t in range(NT):
                tsl = slice(t * T, (t + 1) * T)
                bank_pool[0] = pool_all

                def load_rope(s, t=t):
                    rp_t, rp_r = ropet[s % 2]
                    r0 = t * T + s * 128
                    P.dma(lambda e, rp_t=rp_t, r0=r0: e.dma_start(out=rp_t[:], in_=rope_d[r0:r0 + 128, 256:768]), rp_r, writes=[rp_r])
                    return rp_t, rp_r

                g_idx = l * NT + t
                xc, r_xc = xts[g_idx % 2]
                if g_idx not in norm_done:
                    emit_norm(l, t)
                w0 = load_w(0)

                w1 = load_w(1)
                P.dma(lambda e, t=t: e.dma_start(out=ropeA[:].rearrange("p k s c -> p (k s c)"), in_=ropeA_d[t]), r_ropeA, writes=[r_ropeA])

                def rope_batch(src, src_rs, dst_t, dst_r, kc, ks, scale):
                    v = src.rearrange("p s (g d) -> p (s g) d", d=64)
                    dv = dst_t[:].rearrange("p s (g d) -> p (s g) d", d=64)
                    cs = ropeA[:, kc].rearrange("p s (g f) -> p (s g) f", f=8)
                    sn = ropeA[:, ks].rearrange("p s (g f) -> p (s g) f", f=8)
                    a1 = rt1[:].rearrange("p (g f) -> p g f", f=8)
                    a2 = rt2[:].rearrange("p (g f) -> p g f", f=8)
                    P.op("act", lambda e: e.activation(out=dst_t[:], in_=src, func=ACTF.Copy, scale=scale), src_rs, [dst_r])
                    P.op("dve", lambda e: e.tensor_tensor(out=a1, in0=v[:, :, 0:8], in1=cs, op=ALU.mult), src_rs + [r_ropeA], [r_rt1])
                    P.op("dve", lambda e: e.tensor_tensor(out=a2, in0=v[:, :, 8:16], in1=sn, op=ALU.mult), src_rs + [r_ropeA], [r_rt2])
                    P.op("dve", lambda e: e.tensor_tensor(out=dv[:, :, 0:8], in0=a1, in1=a2, op=ALU.subtract), [r_rt1, r_rt2, dst_r], [dst_r])
                    P.op("dve", lambda e: e.tensor_tensor(out=a1, in0=v[:, :, 8:16], in1=cs, op=ALU.mult), src_rs + [r_ropeA, r_rt1], [r_rt1])
                    P.op("dve", lambda e: e.tensor_tensor(out=a2, in0=v[:, :, 0:8], in1=sn, op=ALU.mult), src_rs + [r_ropeA, r_rt2], [r_rt2])
                    P.op("dve", lambda e: e.tensor_tensor(out=dv[:, :, 8:16], in0=a1, in1=a2, op=ALU.add), [r_rt1, r_rt2, dst_r], [dst_r])

                for s in range(4):
                    bk, br = proj_tok(w0[0], w0[1], s)
                    P.op("act", lambda e, bk=bk, s=s: e.activation(out=q32[:, s, :], in_=bk[:], func=ACTF.Copy), [br], [r_q32])
                w2 = load_w(2)
                for s in range(4):
                    bk, br = proj_tok(w1[0], w1[1], s)
                    P.op("act", lambda e, bk=bk, s=s: e.activation(out=sx[:, s, :], in_=bk[:], func=ACTF.Copy), [br], [r_sx[s]])
                w3 = load_w(3)
                rope_batch(q32[:], [r_q32], qkb4[0][0], qkb4[0][1], 0, 1, 0.125)
                for s in range(4):
                    bk, br = proj_tok(w2[0], w2[1], s)
                    P.op("act", lambda e, bk=bk, s=s: e.activation(out=vsb[:, s, :], in_=bk[:], func=ACTF.Copy), [br], [r_vsb])
                rope_batch(sx[:], r_sx, qkb4[1][0], qkb4[1][1], 2, 3, 1.0)
                for s in range(4):
                    bk, br = nb()
                    for h in range(4):
                        P.op("pe", lambda e, bk=bk, h=h, s=s: e.matmul(bk[:, h * 128:(h + 1) * 128], lhsT=qkb4[0][0][:, s, h * 128:(h + 1) * 128], rhs=identb[:],
                                                                  start=True, stop=True), [qkb4[0][1], r_identb], [br])
                    P.op("act", lambda e, bk=bk, s=s: e.activation(out=qt[:, :, s * 128:(s + 1) * 128], in_=bk[:].rearrange("p (h n) -> p h n", n=128), func=ACTF.Copy),
                         [br], [r_qt])
                for s in range(4):
                    bk, br = nb()
                    for h in range(4):
                        P.op("pe", lambda e, bk=bk, h=h, s=s: e.matmul(bk[:, h * 128:(h + 1) * 128], lhsT=qkb4[1][0][:, s, h * 128:(h + 1) * 128], rhs=identb[:],
                                                                  start=True, stop=True), [qkb4[1][1], r_identb], [br])
                    P.op("act", lambda e, bk=bk, s=s: e.activation(out=ktsb[0:64, :, 0, s * 128:(s + 1) * 128], in_=bk[0:64, :].rearrange("p (h n) -> p h n", n=128), func=ACTF.Copy),
                         [br], [r_ktsb])
                    P.op("dve", lambda e, bk=bk, s=s: e.tensor_copy(out=ktsb[64:128, :, 1, s * 128:(s + 1) * 128], in_=bk[64:128, :].rearrange("p (h n) -> p h n", n=128)),
                         [br, r_ktsb], [r_ktsb])
                P.dma(lambda e, l=l, tsl=tsl: e.dma_start(out=ktc_d[l][:, :, :, tsl], in_=ktsb[:]), r_ktsb, reads=[r_ktsb], writes=[r_kc[l][t]], eng=STQ)
                P.dma(lambda e, l=l, tsl=tsl: e.dma_start(out=vc_d[l][tsl, :].rearrange("(i p) n -> p i n", p=128), in_=vsb[:]), r_vsb,
                      reads=[r_vsb], writes=[r_vc[l][t]], eng=STQ)

                def silu_gate(bk, br, out_ap, out_res, k):
                    e_t, e_r = sq[k % 2]
                    P.op("act", lambda e: e.activation(out=e_t[:], in_=bk[:], func=ACTF.Exp, scale=-1.0), [br], [e_r])
                    P.op("dve", lambda e: e.tensor_scalar(out=e_t[:], in0=e_t[:], scalar1=1.0, scalar2=None, op0=ALU.add), [e_r], [e_r])
                    P.op("dve", lambda e: e.reciprocal(out=e_t[:], in_=e_t[:]), [e_r], [e_r])
                    P.op("dve", lambda e: e.tensor_tensor(out=out_ap, in0=bk[:], in1=e_t[:], op=ALU.mult), [br, e_r], [out_res])

                def gen_B(t=t, w3=w3):
                    nxt = w3
                    w_t, w_r = nxt
                    nxt = load_w(4)
                    for jj in range(4):
                        bk, br = proj_feat(w_t, w_r, jj)
                        silu_gate(bk, br, za[:, jj, :], r_za, jj)
                        yield
                    w_t, w_r = nxt
                    nxt = load_w(6)
                    for jj in range(4):
                        bk, br = proj_feat(w_t, w_r, jj)
                        P.op("act", lambda e, bk=bk, jj=jj: e.activation(out=fA[:, jj, :], in_=bk[:], func=ACTF.Copy), [br], [r_fA])
                        yield
                    w_t, w_r = nxt
                    nxt = load_w(5)
                    for jj in range(4):
                        bk, br = proj_feat(w_t, w_r, jj)
                        P.op("pool", lambda e, jj=jj: e.tensor_copy(out=uext[:, jj, 0:2], in_=uext[:, jj, T:T + 2]), [r_uext], [r_uext])
                        P.op("dve", lambda e, bk=bk, jj=jj: e.tensor_tensor(out=uext[:, jj, 2:T + 2], in0=bk[:], in1=fA[:, jj, :], op=ALU.mult), [br, r_fA, r_uext], [r_uext])
                        P.op("dve", lambda e, jj=jj, cw_t=cw_t: e.tensor_scalar(out=fA[:, jj, :], in0=uext[:, jj, 2:T + 2], scalar1=cw_t[:, jj * 3 + 2:jj * 3 + 3], scalar2=None, op0=ALU.mult),
                             [r_uext, cw_r], [r_fA])
                        P.op("dve", lambda e, jj=jj, cw_t=cw_t: e.scalar_tensor_tensor(out=fA[:, jj, :], in0=uext[:, jj, 1:T + 1], scalar=cw_t[:, jj * 3 + 1:jj * 3 + 2], in1=fA[:, jj, :],
                                                                            op0=ALU.mult, op1=ALU.add), [r_uext, cw_r, r_fA], [r_fA])
                        P.op("dve", lambda e, jj=jj, cw_t=cw_t: e.scalar_tensor_tensor(out=fA[:, jj, :], in0=uext[:, jj, 0:T], scalar=cw_t[:, jj * 3:jj * 3 + 1], in1=fA[:, jj, :],
                                                                            op0=ALU.mult, op1=ALU.add), [r_uext, cw_r, r_fA], [r_fA])
                        yield
                    w_t, w_r = nxt
                    nxt = load_w(7)
                    for jj in range(4):
                        bk, br = proj_feat(w_t, w_r, jj)
                        P.op("dve", lambda e, bk=bk, jj=jj: e.tensor_tensor(out=fA[:, jj, :], in0=bk[:], in1=fA[:, jj, :], op=ALU.mult), [br, r_fA], [r_fA])
                        yield
                    w_t, w_r = nxt
                    nxt = load_w(8)
                    for jj in range(4):
                        bk, br = proj_feat(w_t, w_r, jj)
                        silu_gate(bk, br, fB[:, jj, :], r_fB, jj)
                        P.op("pool", lambda e, jj=jj: e.tensor_tensor(out=ct[:, jj, :], in0=fA[:, jj, :], in1=fB[:, jj, :], op=ALU.mult), [r_fA, r_fB], [r_ct])
                        yield
                    w8 = nxt
                    w9 = load_w(9)
                    for s in range(4):
                        bk, br = proj_tok(w9[0], w9[1], s)
                        P.op("act", lambda e, bk=bk, s=s: e.activation(out=vr[:, s, :], in_=bk[:], func=ACTF.Copy), [br], [r_vr])
                        yield
                    w_t, w_r = w8
                    for s in range(4):
                        rp_t, rp_r = load_rope(s)
                        bk, br = proj_tok(w_t, w_r, s)
                        if s == 3:
                            nxt = load_w(10)
                        yield
                        P.op("act", lambda e, bk=bk: e.activation(out=qk32[:], in_=bk[:], func=ACTF.Copy), [br], [r_qk32])
                        v = qk32[:].rearrange("p (g d) -> p g d", d=64)
                        dv = rqk[:].rearrange("p (g d) -> p g d", d=64)
                        cs = rp_t[:, 0:256].rearrange("p (g f) -> p g f", f=32)
                        sn = rp_t[:, 256:512].rearrange("p (g f) -> p g f", f=32)
                        a1 = rt1[:].rearrange("p (g f) -> p g f", f=32)
                        a2 = rt2[:].rearrange("p (g f) -> p g f", f=32)
                        P.op("dve", lambda e, v=v, cs=cs, a1=a1: e.tensor_tensor(out=a1, in0=v[:, :, 0:32], in1=cs, op=ALU.mult), [r_qk32, rp_r], [r_rt1])
                        P.op("dve", lambda e, v=v, sn=sn, a2=a2: e.tensor_tensor(out=a2, in0=v[:, :, 32:64], in1=sn, op=ALU.mult), [r_qk32, rp_r], [r_rt2])
                        P.op("dve", lambda e, dv=dv, a1=a1, a2=a2: e.tensor_tensor(out=dv[:, :, 0:32], in0=a1, in1=a2, op=ALU.subtract), [r_rt1, r_rt2], [r_rqk])
                        P.op("dve", lambda e, v=v, cs=cs, a1=a1: e.tensor_tensor(out=a1, in0=v[:, :, 32:64], in1=cs, op=ALU.mult), [r_qk32, rp_r, r_rt1], [r_rt1])
                        P.op("dve", lambda e, v=v, sn=sn, a2=a2: e.tensor_tensor(out=a2, in0=v[:, :, 0:32], in1=sn, op=ALU.mult), [r_qk32, rp_r, r_rt2], [r_rt2])
                        P.op("dve", lambda e, dv=dv, a1=a1, a2=a2: e.tensor_tensor(out=dv[:, :, 32:64], in0=a1, in1=a2, op=ALU.add), [r_rt1, r_rt2, r_rqk], [r_rqk])
                        P.op("pool", lambda e: e.tensor_copy(out=rb4[:, 0, :], in_=rqk[:, 0:256]), [r_rqk], [r_rb4])
                        P.op("pool", lambda e: e.tensor_copy(out=rb4[:, 1, :], in_=rqk[:, 256:512]), [r_rqk, r_rb4], [r_rb4])
                        P.op("pool", lambda e: e.tensor_tensor(out=rb4[:, 2, :], in0=rqk[:, 0:256], in1=xiQ[:], op=ALU.mult), [r_rqk, r_xiQ, r_rb4], [r_rb4])
                        P.op("pool", lambda e: e.tensor_tensor(out=rb4[:, 3, :], in0=rqk[:, 256:512], in1=zetaK[:], op=ALU.mult), [r_rqk, r_zetaK, r_rb4], [r_rb4])
                        yield
                        yield
                        for k in range(3):
                            bk2, br2 = nb()
                            for h in range(4):
                                P.op("pe", lambda e, bk2=bk2, h=h, k=k: e.matmul(bk2[0:64, h * 128:(h + 1) * 128], lhsT=rb4[:, k, h * 64:(h + 1) * 64], rhs=identb[:],
                                                                               start=True, stop=True), [r_rb4, r_identb], [br2])
                            P.op("act" if k != 1 else "dve",
                                 (lambda e, bk2=bk2, k=k: e.activation(out=rtr[:, k, :], in_=bk2[0:64, :], func=ACTF.Copy)) if k != 1 else
                                 (lambda e, bk2=bk2, k=k: e.tensor_copy(out=rtr[:, k, :], in_=bk2[0:64, :])), [br2], [r_rtr])
                        yield
                        yield
                        bs, brs = nb()
                        for h in range(4):
                            P.op("pe", lambda e, bs=bs, h=h: e.matmul(bs[:, h * 128:(h + 1) * 128], lhsT=rtr[:, 1, h * 128:(h + 1) * 128], rhs=rtr[:, 0, h * 128:(h + 1) * 128],
                                                                      start=True, stop=True), [r_rtr], [brs])
                        P.op("dve", lambda e, bs=bs: e.tensor_tensor(out=smT[:], in0=bs[:], in1=dmT[:], op=ALU.mult), [brs, r_dmT], [r_smT])
                        bkv, brkv = nb()
                        for h in range(4):
                            P.op("pe", lambda e, bkv=bkv, h=h, s=s: e.matmul(bkv[0:64, h * 128:(h + 1) * 128], lhsT=rb4[:, 3, h * 64:(h + 1) * 64], rhs=vr[:, s, h * 128:(h + 1) * 128],
                                                                             start=True, stop=True), [r_rb4, r_vr], [brkv])
                        yield
                        yield
                        bo, bro = nb()
                        for h in range(4):
                            P.op("pe", lambda e, bo=bo, h=h, s=s: e.matmul(bo[:, h * 128:(h + 1) * 128], lhsT=vr[:, s, h * 128:(h + 1) * 128], rhs=smT[:, h * 128:(h + 1) * 128],
                                                                           start=True, stop=False), [r_vr, r_smT], [bro])
                            P.op("pe", lambda e, bo=bo, h=h: e.matmul(bo[:, h * 128:(h + 1) * 128], lhsT=stateb[:, h * 128:(h + 1) * 128], rhs=rtr[:, 2, h * 128:(h + 1) * 128],
                                                                      start=False, stop=True), [r_stateb, r_rtr], [bro])
                        P.op("act", lambda e, bo=bo, s=s: e.activation(out=fA[:, :, s * 128:(s + 1) * 128], in_=bo[:].rearrange("p (h n) -> p h n", n=128), func=ACTF.Copy),
                             [bro], [r_fA])
                        for h in range(4):
                            P.op("dve", lambda e, bkv=bkv, h=h: e.scalar_tensor_tensor(out=state[:, h * 128:(h + 1) * 128], in0=state[:, h * 128:(h + 1) * 128], scalar=DECAY[h],
                                                                                       in1=bkv[0:64, h * 128:(h + 1) * 128], op0=ALU.mult, op1=ALU.add), [brkv, r_state], [r_state])
                        P.op("dve", lambda e: e.tensor_copy(out=stateb[:], in_=state[:]), [r_state], [r_stateb])
                        yield
                    w_t, w_r = nxt
                    for jj in range(4):
                        bk, br = proj_feat(w_t, w_r, jj)
                        silu_gate(bk, br, fB[:, jj, :], r_fB, jj)
                        yield
                    for h in range(4):
                        sq_t, sq_r = sq[h % 2]
                        P.op("pool", lambda e, sq_t=sq_t, h=h: e.tensor_tensor(out=sq_t[:], in0=fA[:, h, :], in1=fA[:, h, :], op=ALU.mult), [r_fA], [sq_r])
                        bk, br = nb()
                        P.op("pe", lambda e, bk=bk, sq_t=sq_t: e.matmul(bk[:], lhsT=onesf[:], rhs=sq_t[:], start=True, stop=True), [r_onesf, sq_r], [br])
                        rms_scale(bk, 1.0 / 128, s1, r_s1, br)
                        P.op("dve", lambda e, h=h: e.tensor_tensor(out=s4[:], in0=fA[:, h, :], in1=s1[:], op=ALU.mult), [r_fA, r_s1], [r_s4])
                        P.op("pool", lambda e, h=h: e.tensor_tensor(out=rT[:, h, :], in0=s4[:], in1=fB[:, h, :], op=ALU.mult), [r_s4, r_fB], [r_rT])
                        yield

                Sb = [banks[0], banks[1]]
                if t >= 3:
                    OL = [(banks[2], banks[3]), (banks[4], banks[5])]
                    poolB_t = [6, 7]
                else:
                    OL = [(banks[2], banks[3])]
                    poolB_t = [4, 5, 6, 7]
                acc_t = [(s2, r_s2), (s3, r_s3)]

                def gen_A(t=t, l=l, lt=lt, sg_t=sg_t, lam_init=lam_init):
                    tiles = [(h, c, j) for h in range(4) for c in range(2) for j in range(t + 1)]
                    blocks = [(ti, i) for ti in range(len(tiles)) for i in range(4)]
                    kv = {}

                    def load(ti):
                        h, c, j = tiles[ti]
                        kt_t, kt_r = ktb[ti % 3]
                        vt_t, vt_r = vtb[ti % 3]
                        ksl = slice(j * T, (j + 1) * T)
                        P.dma(lambda e: e.dma_start(out=kt_t[:], in_=ktc_d[l][:, h, c, ksl]), kt_r, reads=[r_kc[l][j]], writes=[kt_r])
                        P.dma(lambda e: e.dma_start(out=vt_t[:], in_=vc_d[l][ksl, h * 128:(h + 1) * 128].rearrange("(i p) n -> p i n", p=128)),
                              vt_r, reads=[r_vc[l][j]], writes=[vt_r])
                        kv[ti] = (kt_t, kt_r, vt_t, vt_r)

                    def qk(b):
                        ti, i = blocks[b]
                        h, c, j = tiles[ti]
                        kt_t, kt_r, vt_t, vt_r = kv[ti]
                        q0 = 128 * i if j == t else 0
                        sbk, sbr = Sb[b % 2]
                        pt_t, pt_r = ptb[b % 3]
                        P.op("pe", lambda e: e.matmul(sbk[:, q0:T], lhsT=kt_t[:, i * 128:(i + 1) * 128], rhs=qt[:, h, q0:T], start=True, stop=True), [kt_r, r_qt], [sbr])
                        P.op("act", lambda e: e.activation(out=pt_t[:, q0:T], in_=sbk[:, q0:T], func=ACTF.Exp), [sbr], [pt_r])
                        if j == t:
                            P.op("pool", lambda e: e.tensor_tensor(out=pt_t[:, q0:q0 + 128], in0=pt_t[:, q0:q0 + 128], in1=mask01[:, 0:128], op=ALU.mult), [pt_r, r_mask01], [pt_r])

                    def pv(b):
                        ti, i = blocks[b]
                        h, c, j = tiles[ti]
                        kt_t, kt_r, vt_t, vt_r = kv[ti]
                        q0 = 128 * i if j == t else 0
                        pt_t, pt_r = ptb[b % 3]
                        (Ob, Obr), (Lb, Lbr) = OL[(h * 2 + c) % len(OL)]
                        first = (j == 0 and i == 0)
                        lastb = (j == t and i == 3)
                        P.op("pe", lambda e: e.matmul(Ob[:, q0:T], lhsT=vt_t[:, i, :], rhs=pt_t[:, q0:T], start=first, stop=lastb), [vt_r, pt_r], [Obr])
                        P.op("pe", lambda e: e.matmul(Lb[:, q0:T], lhsT=onesb[:], rhs=pt_t[:, q0:T], start=first, stop=lastb), [r_onesb, pt_r], [Lbr])
                        if lastb:
                            a_t, a_r = acc_t[c]
                            P.op("dve", lambda e: e.reciprocal(out=s1[:], in_=Lb[:]), [Lbr, r_s1], [r_s1])
                            P.op("dve", lambda e: e.tensor_tensor(out=a_t[:], in0=Ob[:], in1=s1[:], op=ALU.mult), [Obr, r_s1], [a_r])
                            if c == 1:
                                P.op("dve", lambda e: e.scalar_tensor_tensor(out=s2[:], in0=s3[:], scalar=lt[:, 0:1], in1=s2[:], op0=ALU.mult, op1=ALU.add), [r_s3, r_s2, lr], [r_s2])
                                P.op("pool", lambda e: e.tensor_tensor(out=s3[:], in0=s2[:], in1=s2[:], op=ALU.mult), [r_s2], [r_s3])
                                def part2(h=h):
                                    bk, br = nb()
                                    P.op("pe", lambda e: e.matmul(bk[:], lhsT=onesf[:], rhs=s3[:], start=True, stop=True), [r_onesf, r_s3], [br])
                                    rms_scale(bk, 1.0 / 128, s1, r_s1, br)
                                    P.op("dve", lambda e: e.scalar_tensor_tensor(out=s2[:], in0=s2[:], scalar=sg_t[:, 0:1], in1=s1[:], op0=ALU.mult, op1=ALU.mult), [r_s2, sg_r, r_s1], [r_s2])
                                    P.op("dve", lambda e: e.scalar_tensor_tensor(out=aT[:, h, :], in0=s2[:], scalar=1.0 - lam_init, in1=za[:, h, :], op0=ALU.mult, op1=ALU.mult),
                                         [r_s2, r_za], [r_aT])
                                deferred.append([min(6, 4 * (t + 1) - 1), part2])

                    deferred = []
                    load(0)
                    if len(tiles) > 1:
                        load(1)
                    nblk = len(blocks)
                    qk(0)
                    for b in range(nblk):
                        if b + 1 < nblk:
                            tn, inn = blocks[b + 1]
                            if inn == 0 and tn + 1 < len(tiles):
                                load(tn + 1)
                            qk(b + 1)
                        for d in list(deferred):
                            d[0] -= 1
                            if d[0] <= 0:
                                deferred.remove(d)
                                d[1]()
                        pv(b)
                        yield
                    for d in deferred:
                        d[1]()

                bank_pool[0] = poolB_t
                nA = 8 * 4 * (t + 1)
                nB = 4 * 5 + 4 + 4 * 8 + 4 + 4
                interleave(gen_A(), gen_B(), nA, nB)
                bank_pool[0] = pool_all

                br_src = [(aT, r_aT), (ct, r_ct), (rT, r_rT)]
                for j in range(8):
                    wp_t, wp_r = wpb[j % 2]
                    P.dma(lambda e, wp_t=wp_t, j=j, l=l: e.dma_start(out=wp_t[:], in_=wpostb_d[l][j]), wp_r, reads=[wres[f"post{l}_{j}"]], writes=[wp_r])
                    gv = wp_t[:, 0:3072].rearrange("p (c i n) -> p c i n", i=3, n=128)
                    bv = wp_t[:, 3072:4608].rearrange("p (i k n) -> p i k n", k=4, n=128)
                    for i in range(3):
                        gb, gbr = nb()
                        for c in range(NCH):
                            P.op("pe", lambda e, gb=gb, c=c, i=i, gv=gv: e.matmul(gb[:], lhsT=gv[:, c, i, :], rhs=ht[:, c, :], start=(c == 0), stop=(c == NCH - 1)), [wp_r, r_ht], [gbr])
                        g_t, g_r = gtb[i]
                        P.op("act", lambda e, gb=gb, g_t=g_t: e.activation(out=g_t[:], in_=gb[:], func=ACTF.Sigmoid), [gbr], [g_r])
                        pb, pbr = nb()
                        s_t, s_r = br_src[i]
                        for k in range(4):
                            P.op("pe", lambda e, pb=pb, k=k, i=i, bv=bv, s_t=s_t: e.matmul(pb[:], lhsT=bv[:, i, k, :], rhs=s_t[:, k, :], start=(k == 0), stop=(k == 3)), [wp_r, s_r], [pbr])
                        if i == 0:
                            P.op("dve", lambda e, pb=pb, g_t=g_t: e.tensor_tensor(out=s4[:], in0=pb[:], in1=g_t[:], op=ALU.mult), [pbr, g_r], [r_s4])
                        else:
                            P.op("dve", lambda e, pb=pb, g_t=g_t: e.tensor_tensor(out=g_t[:], in0=pb[:], in1=g_t[:], op=ALU.mult), [pbr, g_r], [g_r])
                            if i == 1:
                                P.op("pool", lambda e, g_t=g_t: e.tensor_tensor(out=s4[:], in0=s4[:], in1=g_t[:], op=ALU.add), [r_s4, g_r], [r_s4])
                            else:
                                P.op("pool", lambda e, g_t=g_t, j=j: e.tensor_tensor(out=mg[:, j, :], in0=s4[:], in1=g_t[:], op=ALU.add), [r_s4, g_r], [r_mg])
                nxt_tile = (l, t + 1) if t + 1 < NT else ((l + 1, 0) if l + 1 < NL else None)
                if nxt_tile is not None:
                    emit_norm(*nxt_tile)
                for jo in range(8):
                    wo_t, wo_r = wob[jo % 2]
                    P.dma(lambda e, wo_t=wo_t, jo=jo, l=l: e.dma_start(out=wo_t[:].rearrange("p c n -> p (c n)"), in_=woutb_d[l][jo]), wo_r, reads=[wres[f"out{l}_{jo}"]], writes=[wo_r])
                    yb, ybr = nb()
                    for c in range(NCH):
                        P.op("pe", lambda e, yb=yb, c=c, wo_t=wo_t: e.matmul(yb[:], lhsT=wo_t[:, c, :], rhs=mg[:, c, :], start=(c == 0), stop=(c == NCH - 1)), [wo_r, r_mg], [ybr])
                    P.op("dve", lambda e, xc=xc, yb=yb, jo=jo: e.tensor_tensor(out=xc[:, jo, :], in0=yb[:], in1=xc[:, jo, :], op=ALU.add), [ybr, r_xc], [r_xc])
                if DBG and l == 0 and t == DBG - 1:
                    dbgs, r_dbgs = fA[:].rearrange("p a b -> p (a b)"), r_fA
                    for k, (tt, rr) in enumerate([(aT, r_aT), (ct, r_ct), (rT, r_rT), (mg, r_mg), (qt, r_qt)]):
                        P.op("dve", lambda e, tt=tt: e.tensor_copy(out=dbgs, in_=tt[:, 0:4, :].rearrange("p a b -> p (a b)")), [rr], [r_dbgs])
                        final.append(P.dma(lambda e, k=k: e.dma_start(out=dbg_d[:, k, :], in_=dbgs), r_dbgs, reads=[r_dbgs]))
                if not last:
                    P.dma(lambda e, xc=xc, tsl=tsl: e.dma_start(out=x1T_d.rearrange("(c p) t -> p c t", p=128)[:, :, tsl], in_=xc[:]), r_xc, reads=[r_xc], writes=[r_x1[t]], eng=STQ)
                else:
                    bk, br = nb()
                    for c in range(NCH):
                        sq_t, sq_r = sq[c % 2]
                        P.op("pool", lambda e, xc=xc, sq_t=sq_t, c=c: e.tensor_tensor(out=sq_t[:], in0=xc[:, c, :], in1=xc[:, c, :], op=ALU.mult), [r_xc], [sq_r])
                        P.op("pe", lambda e, sq_t=sq_t, c=c, bk=bk: e.matmul(bk[:], lhsT=onesf[:], rhs=sq_t[:], start=(c == 0), stop=(c == NCH - 1)), [r_onesf, sq_r], [br])
                    rms_scale(bk, 1.0 / D, rstd, r_rstd, br)
                    for c in range(NCH):
                        P.op("dve", lambda e, xc=xc, c=c: e.scalar_tensor_tensor(out=xc[:, c, :], in0=xc[:, c, :], scalar=fng[:, c:c + 1], in1=rstd[:], op0=ALU.mult, op1=ALU.mult),
                             [r_xc, r_fng, r_rstd], [r_xc])
                    final.append(P.dma(lambda e, xc=xc, tsl=tsl: e.dma_start(out=outT_d.rearrange("(c p) t -> p c t", p=128)[:, :, tsl], in_=xc[:]), r_xc, reads=[r_xc], eng=STQ))

        P.emit(st, final_waits=final)
        build.stats = (P.stats, P.nwaits)
    return nc


_CACHE = {}


def _prep_inputs(x, norm_g, w_in, attn_lambda, attn_subln_g, conv_w, w_branch, w_out, final_norm_g):
    B, S, _ = x.shape
    NL = w_in.shape[0]
    tabs, _ = _const_tables(S)
    common = dict(tabs)
    common["fng"] = np.ascontiguousarray(np.asarray(final_norm_g, np.float32).reshape(NCH, 128).T)
    for l in range(NL):
        win, wpost, wo = _layout_weights(np.asarray(w_in[l], np.float32), np.asarray(w_branch[l], np.float32), np.asarray(w_out[l], np.float32))
        common[f"win{l}"] = win
        common[f"wpost{l}"] = wpost
        common[f"wout{l}"] = wo
        common[f"ng{l}"] = np.ascontiguousarray(np.asarray(norm_g[l], np.float32).reshape(NCH, 128).T)
        common[f"sg{l}"] = np.ascontiguousarray(np.asarray(attn_subln_g[l], np.float32).reshape(128, 1))
        cwl = np.asarray(conv_w[l], np.float32)
        common[f"cw{l}"] = np.ascontiguousarray(cwl.reshape(3, 4, 128).transpose(2, 1, 0)).reshape(128, 12)
        common[f"al{l}"] = np.ascontiguousarray(np.broadcast_to(np.asarray(attn_lambda[l], np.float32).reshape(1, 256), (128, 256)))
    in_maps = []
    for b in range(B):
        m = dict(common)
        m["xT"] = np.ascontiguousarray(np.asarray(x[b], np.float32).T)
        in_maps.append(m)
    return in_maps, B, S, NL


def kernel(x, norm_g, w_in, attn_lambda, attn_subln_g, conv_w, w_branch, w_out, final_norm_g):
    in_maps, B, S, NL = _prep_inputs(x, norm_g, w_in, attn_lambda, attn_subln_g, conv_w, w_branch, w_out, final_norm_g)
    nc = build(S, NL)
    res = run_bass_kernel_spmd(nc, in_maps, core_ids=list(range(B)))
    if DBG:
        kernel.dbg = res.results[0]["dbg"]
    out = np.stack([np.ascontiguousarray(res.results[b]["outT"].T) for b in range(B)], axis=0)
    return out.astype(np.float32)
```

```python
import math
import os
STOP = int(os.environ.get('KSTOP', '99'))
STQ = os.environ.get('KSTQ', 'pool')
SUB = os.environ.get('KSUB', 'abcdefghi')
DBG = int(os.environ.get('KDBG', '0'))
from contextlib import ExitStack
import numpy as np
import concourse.bass as bass
import concourse.mybir as mybir
from concourse.bass_utils import run_bass_kernel_spmd

F32 = mybir.dt.float32
BF16 = mybir.dt.bfloat16
ALU = mybir.AluOpType
ACTF = mybir.ActivationFunctionType

D = 1024
NCH = 8
T = 512
EPS = 1e-6
ENGINES = ("pe", "act", "dve", "pool", "sp")
EPOCH = 30000


class Res:
    __slots__ = ("name", "writer", "readers", "last_dma")

    def __init__(self, name):
        self.name = name
        self.writer = None
        self.readers = []
        self.last_dma = None


class Op:
    __slots__ = ("eng", "fn", "deps", "milestone", "sem", "val", "is_dma", "dma_res", "idx")

    def __init__(self, eng, fn):
        self.eng = eng
        self.fn = fn
        self.deps = []
        self.milestone = False
        self.sem = None
        self.val = 0
        self.is_dma = False
        self.dma_res = None
        self.idx = 0


class Prog:
    def __init__(self, nc):
        self.nc = nc
        self.ops = {e: [] for e in ENGINES}
        self.n = 0
        self.dma_res = []

    def _add(self, op, reads, writes):
        deps = set()
        for r in reads:
            if r.writer is not None:
                deps.add(r.writer)
        for w in writes:
            if w.writer is not None:
                deps.add(w.writer)
            for rd in w.readers:
                deps.add(rd)
        deps.discard(op)
        best = {}
        keep = []
        for d in deps:
            if d.is_dma:
                keep.append(d)
                continue
            if d.eng == op.eng and not op.is_dma:
                if op.eng == "pe":
                    continue
            if d.eng not in best or best[d.eng].idx < d.idx:
                best[d.eng] = d
        keep.extend(best.values())
        for d in keep:
            d.milestone = True
        op.deps = keep
        for r in reads:
            if op.is_dma:
                r.readers.append(op)
            else:
                r.readers = [x for x in r.readers if x.is_dma or x.eng != op.eng]
                r.readers.append(op)
        for w in writes:
            w.writer = op
            w.readers = []
        op.idx = self.n
        self.n += 1
        self.ops[op.eng].append(op)
        return op

    skip = False

    def op(self, eng, fn, reads=(), writes=()):
        if self.skip:
            return None
        return self._add(Op(eng, fn), list(reads), list(writes))

    def dma(self, fn, sb, reads=(), writes=(), eng="sp"):
        if self.skip:
            return None
        op = Op(eng, fn)
        op.is_dma = True
        op.dma_res = (sb, eng)
        op.milestone = True
        o = self._add(op, list(reads), list(writes))
        if sb.last_dma is not None and sb.last_dma not in o.deps:
            o.deps.append(sb.last_dma)
        sb.last_dma = o
        if (sb, eng) not in self.dma_res:
            self.dma_res.append((sb, eng))
        return o

    def emit(self, stack, final_waits=()):
        nc = self.nc
        for e in ENGINES:
            cnt = 0
            sems = []
            for op in self.ops[e]:
                if op.is_dma or not op.milestone:
                    continue
                ep = cnt // EPOCH
                while len(sems) <= ep:
                    sems.append(stack.enter_context(nc.semaphore(f"s_{e}_{len(sems)}")))
                op.sem = sems[ep]
                op.val = cnt % EPOCH + 1
                cnt += 1
        dsem = {}
        dcnt = {}
        for (r, en) in self.dma_res:
            dsem[(id(r), en)] = stack.enter_context(nc.semaphore(f"d_{r.name}_{en}"))
            dcnt[(id(r), en)] = 0
        allops = sorted([o for e in ENGINES for o in self.ops[e]], key=lambda o: o.idx)
        for op in allops:
            if op.is_dma:
                k = (id(op.dma_res[0]), op.dma_res[1])
                dcnt[k] += 16
                if dcnt[k] > 16 * 3000:
                    raise RuntimeError("dma sem overflow risk " + op.dma_res[0].name)
                op.sem = dsem[k]
                op.val = dcnt[k]
        self.stats = {e: len(self.ops[e]) for e in ENGINES}
        nwaits = {e: 0 for e in ENGINES}

        def run(e, eng):
            waited = {}
            for op in self.ops[e]:
                need = {}
                for d in op.deps:
                    k = id(d.sem)
                    if waited.get(k, 0) >= d.val:
                        continue
                    if k not in need or need[k][1] < d.val:
                        need[k] = (d.sem, d.val)
                for k, (s, v) in need.items():
                    eng.wait_ge(s, v)
                    waited[k] = v
                    nwaits[e] += 1
                ins = op.fn(eng)
                if op.milestone:
                    ins.then_inc(op.sem, 16 if op.is_dma else 1)
            if e == "sp":
                for d in final_waits:
                    eng.wait_ge(d.sem, d.val)

        block = stack.enter_context(nc.Block())

        @block.tensor
        def _(eng):
            run("pe", eng)

        @block.scalar
        def _(eng):
            run("act", eng)

        @block.vector
        def _(eng):
            run("dve", eng)

        @block.gpsimd
        def _(eng):
            run("pool", eng)

        @block.sync
        def _(eng):
            run("sp", eng)

        self.nwaits = nwaits


def _const_tables(S):
    pos = np.arange(S, dtype=np.float64)
    inv = 500000.0 ** (-np.arange(0, 16, 2, dtype=np.float64) / 16.0)
    ang = (pos.astype(np.float32)[:, None] * inv.astype(np.float32)[None, :]).astype(np.float32).astype(np.float64)
    ca, sa = np.cos(ang), np.sin(ang)
    ca8 = np.tile(ca, (1, 8))
    sa8 = np.tile(sa, (1, 8))
    invr = 1.0 / (10000.0 ** np.linspace(0.0, 1.0, 32, dtype=np.float32).astype(np.float64))
    angr = (pos.astype(np.float32)[:, None] * invr.astype(np.float32)[None, :]).astype(np.float32).astype(np.float64)
    cr, sr = np.cos(angr), np.sin(angr)
    crq = np.tile(cr, (1, 4)); srq = np.tile(sr, (1, 4))
    crk = crq * 0.125; srk = srq * 0.125
    rope = np.concatenate([ca8 * 0.125, sa8 * 0.125, ca8, sa8,
                           crq, crk, srq, srk], axis=1)
    rope = rope.astype(np.float32)
    log_g = np.log(1.0 - 2.0 ** (-5.0 - np.arange(4, dtype=np.float64)))
    idx = np.arange(128, dtype=np.float64)
    diff = idx[:, None] - idx[None, :]
    dmask = np.where(diff >= 0, np.exp(np.where(diff >= 0, diff, 0.0)[None] * log_g[:, None, None]), 0.0)
    dmT = np.ascontiguousarray(dmask.transpose(2, 0, 1)).astype(np.float32)
    zeta = np.exp((127 - idx)[None, :] * log_g[:, None])
    xi = np.exp((idx + 1.0)[None, :] * log_g[:, None])
    zetaK = np.repeat(zeta.T[:, :, None], 64, axis=2).reshape(128, 256).astype(np.float32)
    xiQ = np.repeat(xi.T[:, :, None], 64, axis=2).reshape(128, 256).astype(np.float32)
    decay = [float(np.exp(128 * lg)) for lg in log_g]
    m01 = (idx[None, :] >= idx[:, None]).astype(np.float32)
    mask01 = np.ascontiguousarray(np.stack([m01, m01], axis=1))
    ident = np.eye(128, dtype=np.float32)
    NT_ = S // T
    ropeA = np.ascontiguousarray(rope[:NT_ * T, 0:256].reshape(NT_, 4, 128, 4, 64).transpose(0, 2, 3, 1, 4)).reshape(NT_, 128, 1024)
    return dict(rope=rope, ropeA=ropeA, dmT=dmT.reshape(128, 512), zetaK=zetaK, xiQ=xiQ, mask01=mask01.reshape(128, 256),
                ident=ident), decay


def _layout_weights(w_in, w_branch, w_out):
    wi = w_in.reshape(NCH, 128, -1)
    win = wi[:, :, :5632].reshape(NCH, 128, 11, 512).transpose(2, 1, 0, 3)
    win = np.ascontiguousarray(win).reshape(11, 128, NCH * 512)
    gates = wi[:, :, 5632:].reshape(NCH, 128, 3, 8, 128).transpose(3, 1, 0, 2, 4)
    gates = gates.reshape(8, 128, NCH * 3 * 128)
    wb = w_branch.reshape(3, 4, 128, 8, 128).transpose(3, 2, 0, 1, 4)
    wb = wb.reshape(8, 128, 3 * 4 * 128)
    wpost = np.ascontiguousarray(np.concatenate([gates, wb], axis=2))
    wo = np.ascontiguousarray(w_out.reshape(NCH, 128, 8, 128).transpose(2, 1, 0, 3)).reshape(8, 128, NCH * 128)
    return win, wpost, wo


def build(S, NL):
    NT = S // T
    nc = bass.Bass("TRN2", target_bir_lowering=False)
    dt = nc.dram_tensor

    def din(name, shape, dtype=F32):
        return dt(name, list(shape), dtype, kind="ExternalInput").ap()

    xT_d = din("xT", [D, S])
    rope_d = din("rope", [S, 768])
    ropeA_d = din("ropeA", [NT, 128, 1024])
    dmT_d = din("dmT", [128, 512])
    zetaK_d = din("zetaK", [128, 256])
    xiQ_d = din("xiQ", [128, 256])
    mask01_d = din("mask01", [128, 256])
    ident_d = din("ident", [128, 128])
    fng_d = din("fng", [128, NCH])
    win_d = [din(f"win{l}", [11, 128, 4096]) for l in range(NL)]
    wpost_d = [din(f"wpost{l}", [8, 128, 4608]) for l in range(NL)]
    wout_d = [din(f"wout{l}", [8, 128, 1024]) for l in range(NL)]
    ng_d = [din(f"ng{l}", [128, NCH]) for l in range(NL)]
    sg_d = [din(f"sg{l}", [128, 1]) for l in range(NL)]
    cw_d = [din(f"cw{l}", [128, 12]) for l in range(NL)]
    al_d = [din(f"al{l}", [128, 256]) for l in range(NL)]
    outT_d = dt("outT", [D, S], F32, kind="ExternalOutput").ap()
    if DBG:
        dbg_d = dt("dbg", [128, 5, 2048], F32, kind="ExternalOutput").ap()
    winb_d = [dt(f"winb{l}", [11, 128, 4096], BF16, kind="Internal").ap() for l in range(NL)]
    wpostb_d = [dt(f"wpostb{l}", [8, 128, 4608], BF16, kind="Internal").ap() for l in range(NL)]
    woutb_d = [dt(f"woutb{l}", [8, 128, 1024], BF16, kind="Internal").ap() for l in range(NL)]
    x1T_d = dt("x1T", [D, S], F32, kind="Internal").ap()
    ktc_d = [dt(f"ktc{l}", [128, 4, 2, S], BF16, kind="Internal").ap() for l in range(NL)]
    vc_d = [dt(f"vc{l}", [S, 512], BF16, kind="Internal").ap() for l in range(NL)]

    _, DECAY = _const_tables(128)

    st = ExitStack()
    with st:
        P = Prog(nc)

        def sb(name, shape, dtype):
            return st.enter_context(nc.sbuf_tensor(name, list(shape), dtype)), Res(name)

        banks = []
        for i in range(8):
            banks.append((st.enter_context(nc.psum_tensor(f"bank{i}", [128, 512], F32)), Res(f"bank{i}")))
        bank_rr = [0]

        def nb():
            b = banks[bank_rr[0] % 8]
            bank_rr[0] += 1
            return b

        xt, r_xt = sb("xt", [128, NCH, T], F32)
        ht, r_ht = sb("ht", [128, NCH, T], BF16)
        sq = [sb(f"sq{i}", [128, T], F32) for i in range(2)]
        rstd, r_rstd = sb("rstd", [128, T], F32)
        wbuf = [sb(f"wbuf{i}", [128, NCH, 512], BF16) for i in range(2)]
        qt, r_qt = sb("qt", [128, 4, T], BF16)
        ktsb, r_ktsb = sb("ktsb", [128, 4, 2, T], BF16)
        q32, r_q32 = sb("q32", [128, 4, 512], F32)
        qk32, r_qk32 = q32[:, 0, :], r_q32
        rqk, r_rqk = q32[:, 1, :], r_q32
        ropeA, r_ropeA = sb("ropeA_s", [128, 4, 4, 64], F32)
        rt1, r_rt1 = sb("rt1", [128, 256], F32)
        rt2, r_rt2 = sb("rt2", [128, 256], F32)
        ropet = [sb(f"ropet{i}", [128, 512], F32) for i in range(2)]
        za, r_za = sb("za", [128, 4, T], F32)
        fA, r_fA = sb("fA", [128, 4, T], F32)
        uext, r_uext = sb("uext", [128, 4, T + 2], F32)
        fB, r_fB = sb("fB", [128, 4, T], F32)
        ct, r_ct = sb("ct", [128, 4, T], BF16)
        rT, r_rT = sb("rT", [128, 4, T], BF16)
        aT, r_aT = sb("aT", [128, 4, T], BF16)
        rb4, r_rb4 = sb("rb4", [128, 4, 256], BF16)
        rtr, r_rtr = sb("rtr", [64, 3, 512], BF16)
        vr, r_vr = sb("vr", [128, 4, 512], BF16)
        smT, r_smT = sb("smT", [128, 512], BF16)
        state, r_state = sb("state", [64, 512], F32)
        stateb, r_stateb = sb("stateb", [64, 512], BF16)
        ktb = [sb(f"ktb{i}", [128, T], BF16) for i in range(3)]
        vtb = [sb(f"vtb{i}", [128, 4, 128], BF16) for i in range(3)]
        ptb = [sb(f"ptb{i}", [128, T], BF16) for i in range(4)]
        sx, _ = sb("sx", [128, 4, T], F32)
        s1, r_s1 = sx[:, 0, :], Res("s1")
        s2, r_s2 = sx[:, 1, :], Res("s2")
        s3, r_s3 = sx[:, 2, :], Res("s3")
        s4, r_s4 = sx[:, 3, :], Res("s4")
        r_sx = [r_s1, r_s2, r_s3, r_s4]
        wpb = [sb(f"wpb{i}", [128, 4608], BF16) for i in range(2)]
        mg, r_mg = sb("mg", [128, NCH, T], BF16)
        qkb4 = [(mg[:, 0:4, :], r_mg), (mg[:, 4:8, :], r_mg)]
        gtb = [(s1, r_s1), (s2, r_s2), (s3, r_s3)]
        vsb, r_vsb = aT, r_aT
        xt2, r_xt2 = sb("xt2", [128, NCH, T], F32)
        xts = [(xt, r_xt), (xt2, r_xt2)]
        wob = [sb(f"wob{i}", [128, NCH, 128], BF16) for i in range(2)]
        dmT, r_dmT = sb("dmT_s", [128, 512], F32)
        zetaK, r_zetaK = sb("zetaK_s", [128, 256], F32)
        xiQ, r_xiQ = sb("xiQ_s", [128, 256], F32)
        mask01, r_mask01 = sb("mask01_s", [128, 256], F32)
        identf, r_identf = sb("identf", [128, 128], F32)
        identb, r_identb = sb("identb", [128, 128], BF16)
        onesf, r_onesf = sb("onesf", [128, 128], F32)
        onesb, r_onesb = sb("onesb", [128, 128], BF16)
        ng = [sb(f"ng_s{l}", [128, NCH], F32) for l in range(NL)]
        fng, r_fng = sb("fng_s", [128, NCH], F32)
        sg = [sb(f"sg_s{l}", [128, 1], F32) for l in range(NL)]
        cw = [sb(f"cw_s{l}", [128, 12], F32) for l in range(NL)]
        al = [sb(f"al_s{l}", [128, 256], F32) for l in range(NL)]
        lamt = [sb(f"lam_s{l}", [128, 8], F32) for l in range(NL)]

        build.sbuf_free = nc.sbuf_bytes_remaining

        def ld(tile_res, src):
            t, r = tile_res
            return P.dma(lambda e, t=t, src=src: e.dma_start(out=t[:], in_=src), r, writes=[r])

        ld((dmT, r_dmT), dmT_d[:, :])
        ld((zetaK, r_zetaK), zetaK_d[:, :])
        ld((xiQ, r_xiQ), xiQ_d[:, :])
        ld((mask01, r_mask01), mask01_d[:, :])
        ld((identf, r_identf), ident_d[:, :])
        ld((fng, r_fng), fng_d[:, :])
        for l in range(NL):
            ld(ng[l], ng_d[l][:, :])
            ld(sg[l], sg_d[l][:, :])
            ld(cw[l], cw_d[l][:, :])
            ld(al[l], al_d[l][:, :])
        P.op("dve", lambda e: e.tensor_copy(out=identb[:], in_=identf[:]), [r_identf], [r_identb])
        P.op("pool", lambda e: e.memset(onesf[:], 1.0), [], [r_onesf])
        P.op("pool", lambda e: e.memset(onesb[:], 1.0), [], [r_onesb])
        P.op("pool", lambda e: e.memset(ktsb[:], 0.0), [], [r_ktsb])
        P.op("pool", lambda e: e.memset(uext[:], 0.0), [], [r_uext])

        for l in range(NL):
            lam_init = 0.8 - 0.6 * math.exp(-0.3 * l)
            a_t, a_r = al[l]
            lt, lr = lamt[l]
            P.op("pool", lambda e, lt=lt: e.memset(lt[:], 0.0), [], [lr])
            P.op("dve", lambda e, a_t=a_t: e.tensor_tensor(out=rt1[:, 0:64], in0=a_t[:, 0:64], in1=a_t[:, 64:128], op=ALU.mult), [a_r], [r_rt1])
            P.op("dve", lambda e, a_t=a_t: e.tensor_tensor(out=rt1[:, 64:128], in0=a_t[:, 128:192], in1=a_t[:, 192:256], op=ALU.mult), [a_r, r_rt1], [r_rt1])
            P.op("act", lambda e, lt=lt: e.activation(out=rt2[:, 0:64], in_=rt1[:, 0:64], func=ACTF.Copy, accum_out=lt[:, 1:2]), [r_rt1], [r_rt2, lr])
            P.op("act", lambda e, lt=lt: e.activation(out=rt2[:, 64:128], in_=rt1[:, 64:128], func=ACTF.Copy, accum_out=lt[:, 2:3]), [r_rt1, r_rt2, lr], [r_rt2, lr])
            P.op("act", lambda e, lt=lt: e.activation(out=lt[:, 3:5], in_=lt[:, 1:3], func=ACTF.Exp), [lr], [lr])
            P.op("dve", lambda e, lt=lt, li=lam_init: e.scalar_tensor_tensor(out=lt[:, 0:1], in0=lt[:, 4:5], scalar=-li, in1=lt[:, 3:4], op0=ALU.add, op1=ALU.subtract), [lr], [lr])

        xt_flat = xt[:].rearrange("p c t -> p (c t)")
        ht_flat = ht[:].rearrange("p c t -> p (c t)")
        wres = {}
        cast_i = [0]

        xt2_flat = xt2[:].rearrange("p c t -> p (c t)")
        mg_flat = mg[:].rearrange("p c t -> p (c t)")
        pipes = [(xt_flat, r_xt, ht_flat, r_ht, "dve"), (xt2_flat, r_xt2, mg_flat, r_mg, "act")]

        def cast_block(src, dst, n, key):
            f_ap, f_r, b_ap, b_r, eng = pipes[cast_i[0] % 2]
            cast_i[0] += 1
            P.dma(lambda e: e.dma_start(out=f_ap[:, 0:n], in_=src), f_r, writes=[f_r])
            if eng == "act":
                P.op("act", lambda e: e.activation(out=b_ap[:, 0:n], in_=f_ap[:, 0:n], func=ACTF.Copy), [f_r], [b_r])
            else:
                P.op(eng, lambda e: e.tensor_copy(out=b_ap[:, 0:n], in_=f_ap[:, 0:n]), [f_r], [b_r])
            r = wres.setdefault(key, Res("w_" + key))
            P.dma(lambda e: e.dma_start(out=dst, in_=b_ap[:, 0:n]), b_r, reads=[b_r], writes=[r], eng=STQ)

        for l in range(NL):
            for g in range(11):
                cast_block(win_d[l][g], winb_d[l][g], 4096, f"in{l}_{g}")
            for j in range(8):
                cast_block(wpost_d[l][j][:, 0:3072], wpostb_d[l][j][:, 0:3072], 3072, f"post{l}_{j}")
                cast_block(wpost_d[l][j][:, 3072:4608], wpostb_d[l][j][:, 3072:4608], 1536, f"post{l}_{j}")
            for jo in range(8):
                cast_block(wout_d[l][jo], woutb_d[l][jo], 1024, f"out{l}_{jo}")

        r_x1 = [Res(f"x1_{t}") for t in range(NT)]
        r_kc = [[Res(f"kc{l}_{t}") for t in range(NT)] for l in range(NL)]
        r_vc = [[Res(f"vc{l}_{t}") for t in range(NT)] for l in range(NL)]
        final = []

        wb_i = [0]

        def rms_scale(src_bank, scale, out_tile, out_res, bres):
            P.op("act", lambda e: e.activation(out=out_tile[:], in_=src_bank[:], func=ACTF.Ln, bias=EPS, scale=scale), [bres], [out_res])
            P.op("act", lambda e: e.activation(out=out_tile[:], in_=out_tile[:], func=ACTF.Exp, scale=-0.5), [out_res], [out_res])

        pool_all = list(range(8))
        pool_B = [4, 5, 6, 7]
        bank_pool = [pool_all]

        def nb():
            lst = bank_pool[0]
            b = banks[lst[bank_rr[0] % len(lst)]]
            bank_rr[0] += 1
            return b

        def interleave(ga, gb, na, nb_):
            ia = ib = 0
            a_done = b_done = False
            while not (a_done and b_done):
                take_a = (not a_done) and (b_done or ia * max(nb_, 1) <= ib * max(na, 1))
                if take_a:
                    try:
                        next(ga)
                        ia += 1
                    except StopIteration:
                        a_done = True
                else:
                    try:
                        next(gb)
                        ib += 1
                    except StopIteration:
                        b_done = True

        norm_done = set()

        def emit_norm(l, t):
            g_idx = l * NT + t
            norm_done.add(g_idx)
            xc, r_xc = xts[g_idx % 2]
            src_d = xT_d if l == 0 else x1T_d
            ng_t, ng_r = ng[l]
            tsl = slice(t * T, (t + 1) * T)
            rd = [r_x1[t]] if l > 0 else []
            P.dma(lambda e: e.dma_start(out=xc[:], in_=src_d.rearrange("(c p) t -> p c t", p=128)[:, :, tsl]), r_xc, reads=rd, writes=[r_xc])
            bk, br = nb()
            for c in range(NCH):
                sq_t, sq_r = sq[c % 2]
                P.op("pool", lambda e, sq_t=sq_t, c=c: e.tensor_tensor(out=sq_t[:], in0=xc[:, c, :], in1=xc[:, c, :], op=ALU.mult), [r_xc], [sq_r])
                P.op("pe", lambda e, sq_t=sq_t, c=c: e.matmul(bk[:], lhsT=onesf[:], rhs=sq_t[:], start=(c == 0), stop=(c == NCH - 1)), [r_onesf, sq_r], [br])
            rms_scale(bk, 1.0 / D, rstd, r_rstd, br)
            for c in range(NCH):
                P.op("dve", lambda e, c=c: e.scalar_tensor_tensor(out=ht[:, c, :], in0=xc[:, c, :], scalar=ng_t[:, c:c + 1], in1=rstd[:],
                                                                   op0=ALU.mult, op1=ALU.mult), [r_xc, ng_r, r_rstd], [r_ht])

        for l in range(NL):
            lam_init = 0.8 - 0.6 * math.exp(-0.3 * l)
            src_d = xT_d if l == 0 else x1T_d
            last = (l == NL - 1)
            ng_t, ng_r = ng[l]
            sg_t, sg_r = sg[l]
            cw_t, cw_r = cw[l]
            lt, lr = lamt[l]
            P.op("pool", lambda e: e.memset(state[:], 0.0), [], [r_state])
            P.op("pool", lambda e: e.memset(stateb[:], 0.0), [], [r_stateb])
            P.op("pool", lambda e: e.memset(uext[:], 0.0), [], [r_uext])

            def load_w(g, l=l):
                w_t, w_r = wbuf[wb_i[0] % 2]
                wb_i[0] += 1
                P.dma(lambda e, w_t=w_t, g=g: e.dma_start(out=w_t[:].rearrange("p c n -> p (c n)"), in_=winb_d[l][g]), w_r,
                      reads=[wres[f"in{l}_{g}"]], writes=[w_r])
                return w_t, w_r

            def proj_tok(w_t, w_r, s):
                bk, br = nb()
                for c in range(NCH):
                    P.op("pe", lambda e, bk=bk, c=c, s=s, w_t=w_t: e.matmul(bk[:], lhsT=ht[:, c, s * 128:(s + 1) * 128], rhs=w_t[:, c, :],
                                                                            start=(c == 0), stop=(c == NCH - 1)), [r_ht, w_r], [br])
                return bk, br

            def proj_feat(w_t, w_r, jj):
                bk, br = nb()
                for c in range(NCH):
                    P.op("pe", lambda e, bk=bk, c=c, jj=jj, w_t=w_t: e.matmul(bk[:], lhsT=w_t[:, c, jj * 128:(jj + 1) * 128], rhs=ht[:, c, :],
                                                                              start=(c == 0), stop=(c == NCH - 1)), [r_ht, w_r], [br])
                return bk, br

            def rope_attn(bk, br, rp_t, rp_r, co, so, dst_t, dst_r, scale):
                v = qk32[:].rearrange("p (g d) -> p g d", d=64)
                dv = dst_t[:].rearrange("p (g d) -> p g d", d=64)
                cs = rp_t[:, co:co + 64].rearrange("p (g f) -> p g f", f=8)
                sn = rp_t[:, so:so + 64].rearrange("p (g f) -> p g f", f=8)
                a1 = rt1[:, 0:64].rearrange("p (g f) -> p g f", f=8)
                a2 = rt2[:, 0:64].rearrange("p (g f) -> p g f", f=8)
                P.op("act", lambda e: e.activation(out=qk32[:], in_=bk[:], func=ACTF.Copy), [br], [r_qk32])
                P.op("pool", lambda e: e.tensor_scalar(out=dst_t[:], in0=qk32[:], scalar1=scale, scalar2=None, op0=ALU.mult), [r_qk32], [dst_r])
                P.op("dve", lambda e: e.tensor_tensor(out=a1, in0=v[:, :, 0:8], in1=cs, op=ALU.mult), [r_qk32, rp_r], [r_rt1])
                P.op("dve", lambda e: e.tensor_tensor(out=a2, in0=v[:, :, 8:16], in1=sn, op=ALU.mult), [r_qk32, rp_r], [r_rt2])
                P.op("dve", lambda e: e.tensor_tensor(out=dv[:, :, 0:8], in0=a1, in1=a2, op=ALU.subtract), [r_rt1, r_rt2, dst_r], [dst_r])
                P.op("dve", lambda e: e.tensor_tensor(out=a1, in0=v[:, :, 8:16], in1=cs, op=ALU.mult), [r_qk32, rp_r, r_rt1], [r_rt1])
                P.op("dve", lambda e: e.tensor_tensor(out=a2, in0=v[:, :, 0:8], in1=sn, op=ALU.mult), [r_qk32, rp_r, r_rt2], [r_rt2])
                P.op("dve", lambda e: e.tensor_tensor(out=dv[:, :, 8:16], in0=a1, in1=a2, op=ALU.add), [r_rt1, r_rt2, dst_r], [dst_r])

            def transpose4(src_t, src_r, dst_t, dst_r, s):
                bk, br = nb()
                for h in range(4):
                    P.op("pe", lambda e, bk=bk, h=h: e.matmul(bk[:, h * 128:(h + 1) * 128], lhsT=src_t[:, h * 128:(h + 1) * 128], rhs=identb[:],
                                                              start=True, stop=True), [src_r, r_identb], [br])
                P.op("act", lambda e, bk=bk: e.activation(out=dst_t[:, :, s * 128:(s + 1) * 128], in_=bk[:].rearrange("p (h n) -> p h n", n=128), func=ACTF.Copy),
                     [br], [dst_r])

            for t in range(NT):
                tsl = slice(t * T, (t + 1) * T)
                bank_pool[0] = pool_all

                def load_rope(s, t=t):
                    rp_t, rp_r = ropet[s % 2]
                    r0 = t * T + s * 128
                    P.dma(lambda e, rp_t=rp_t, r0=r0: e.dma_start(out=rp_t[:], in_=rope_d[r0:r0 + 128, 256:768]), rp_r, writes=[rp_r])
                    return rp_t, rp_r

                g_idx = l * NT + t
                xc, r_xc = xts[g_idx % 2]
                if g_idx not in norm_done:
                    emit_norm(l, t)
                w0 = load_w(0)

                w1 = load_w(1)
                P.dma(lambda e, t=t: e.dma_start(out=ropeA[:].rearrange("p k s c -> p (k s c)"), in_=ropeA_d[t]), r_ropeA, writes=[r_ropeA])

                def rope_batch(src, src_rs, dst_t, dst_r, kc, ks, scale):
                    v = src.rearrange("p s (g d) -> p (s g) d", d=64)
                    dv = dst_t[:].rearrange("p s (g d) -> p (s g) d", d=64)
                    cs = ropeA[:, kc].rearrange("p s (g f) -> p (s g) f", f=8)
                    sn = ropeA[:, ks].rearrange("p s (g f) -> p (s g) f", f=8)
                    a1 = rt1[:].rearrange("p (g f) -> p g f", f=8)
                    a2 = rt2[:].rearrange("p (g f) -> p g f", f=8)
                    P.op("act", lambda e: e.activation(out=dst_t[:], in_=src, func=ACTF.Copy, scale=scale), src_rs, [dst_r])
                    P.op("dve", lambda e: e.tensor_tensor(out=a1, in0=v[:, :, 0:8], in1=cs, op=ALU.mult), src_rs + [r_ropeA], [r_rt1])
                    P.op("dve", lambda e: e.tensor_tensor(out=a2, in0=v[:, :, 8:16], in1=sn, op=ALU.mult), src_rs + [r_ropeA], [r_rt2])
                    P.op("dve", lambda e: e.tensor_tensor(out=dv[:, :, 0:8], in0=a1, in1=a2, op=ALU.subtract), [r_rt1, r_rt2, dst_r], [dst_r])
                    P.op("dve", lambda e: e.tensor_tensor(out=a1, in0=v[:, :, 8:16], in1=cs, op=ALU.mult), src_rs + [r_ropeA, r_rt1], [r_rt1])
                    P.op("dve", lambda e: e.tensor_tensor(out=a2, in0=v[:, :, 0:8], in1=sn, op=ALU.mult), src_rs + [r_ropeA, r_rt2], [r_rt2])
                    P.op("dve", lambda e: e.tensor_tensor(out=dv[:, :, 8:16], in0=a1, in1=a2, op=ALU.add), [r_rt1, r_rt2, dst_r], [dst_r])

                for s in range(4):
                    bk, br = proj_tok(w0[0], w0[1], s)
                    P.op("act", lambda e, bk=bk, s=s: e.activation(out=q32[:, s, :], in_=bk[:], func=ACTF.Copy), [br], [r_q32])
                w2 = load_w(2)
                for s in range(4):
                    bk, br = proj_tok(w1[0], w1[1], s)
                    P.op("act", lambda e, bk=bk, s=s: e.activation(out=sx[:, s, :], in_=bk[:], func=ACTF.Copy), [br], [r_sx[s]])
                w3 = load_w(3)
                rope_batch(q32[:], [r_q32], qkb4[0][0], qkb4[0][1], 0, 1, 0.125)
                for s in range(4):
                    bk, br = proj_tok(w2[0], w2[1], s)
                    P.op("act", lambda e, bk=bk, s=s: e.activation(out=vsb[:, s, :], in_=bk[:], func=ACTF.Copy), [br], [r_vsb])
                rope_batch(sx[:], r_sx, qkb4[1][0], qkb4[1][1], 2, 3, 1.0)
                for s in range(4):
                    bk, br = nb()
                    for h in range(4):
                        P.op("pe", lambda e, bk=bk, h=h, s=s: e.matmul(bk[:, h * 128:(h + 1) * 128], lhsT=qkb4[0][0][:, s, h * 128:(h + 1) * 128], rhs=identb[:],
                                                                  start=True, stop=True), [qkb4[0][1], r_identb], [br])
                    P.op("act", lambda e, bk=bk, s=s: e.activation(out=qt[:, :, s * 128:(s + 1) * 128], in_=bk[:].rearrange("p (h n) -> p h n", n=128), func=ACTF.Copy),
                         [br], [r_qt])
                for s in range(4):
                    bk, br = nb()
                    for h in range(4):
                        P.op("pe", lambda e, bk=bk, h=h, s=s: e.matmul(bk[:, h * 128:(h + 1) * 128], lhsT=qkb4[1][0][:, s, h * 128:(h + 1) * 128], rhs=identb[:],
                                                                  start=True, stop=True), [qkb4[1][1], r_identb], [br])
                    P.op("act", lambda e, bk=bk, s=s: e.activation(out=ktsb[0:64, :, 0, s * 128:(s + 1) * 128], in_=bk[0:64, :].rearrange("p (h n) -> p h n", n=128), func=ACTF.Copy),
                         [br], [r_ktsb])
                    P.op("dve", lambda e, bk=bk, s=s: e.tensor_copy(out=ktsb[64:128, :, 1, s * 128:(s + 1) * 128], in_=bk[64:128, :].rearrange("p (h n) -> p h n", n=128)),
                         [br, r_ktsb], [r_ktsb])
                P.dma(lambda e, l=l, tsl=tsl: e.dma_start(out=ktc_d[l][:, :, :, tsl], in_=ktsb[:]), r_ktsb, reads=[r_ktsb], writes=[r_kc[l][t]], eng=STQ)
                P.dma(lambda e, l=l, tsl=tsl: e.dma_start(out=vc_d[l][tsl, :].rearrange("(i p) n -> p i n", p=128), in_=vsb[:]), r_vsb,
                      reads=[r_vsb], writes=[r_vc[l][t]], eng=STQ)

                def silu_gate(bk, br, out_ap, out_res, k):
                    e_t, e_r = sq[k % 2]
                    P.op("act", lambda e: e.activation(out=e_t[:], in_=bk[:], func=ACTF.Exp, scale=-1.0), [br], [e_r])
                    P.op("dve", lambda e: e.tensor_scalar(out=e_t[:], in0=e_t[:], scalar1=1.0, scalar2=None, op0=ALU.add), [e_r], [e_r])
                    P.op("dve", lambda e: e.reciprocal(out=e_t[:], in_=e_t[:]), [e_r], [e_r])
                    P.op("dve", lambda e: e.tensor_tensor(out=out_ap, in0=bk[:], in1=e_t[:], op=ALU.mult), [br, e_r], [out_res])

                def gen_B(t=t, w3=w3):
                    nxt = w3
                    w_t, w_r = nxt
                    nxt = load_w(4)
                    for jj in range(4):
                        bk, br = proj_feat(w_t, w_r, jj)
                        silu_gate(bk, br, za[:, jj, :], r_za, jj)
                        yield
                    w_t, w_r = nxt
                    nxt = load_w(6)
                    for jj in range(4):
                        bk, br = proj_feat(w_t, w_r, jj)
                        P.op("act", lambda e, bk=bk, jj=jj: e.activation(out=fA[:, jj, :], in_=bk[:], func=ACTF.Copy), [br], [r_fA])
                        yield
                    w_t, w_r = nxt
                    nxt = load_w(5)
                    for jj in range(4):
                        bk, br = proj_feat(w_t, w_r, jj)
                        P.op("pool", lambda e, jj=jj: e.tensor_copy(out=uext[:, jj, 0:2], in_=uext[:, jj, T:T + 2]), [r_uext], [r_uext])
                        P.op("dve", lambda e, bk=bk, jj=jj: e.tensor_tensor(out=uext[:, jj, 2:T + 2], in0=bk[:], in1=fA[:, jj, :], op=ALU.mult), [br, r_fA, r_uext], [r_uext])
                        P.op("dve", lambda e, jj=jj, cw_t=cw_t: e.tensor_scalar(out=fA[:, jj, :], in0=uext[:, jj, 2:T + 2], scalar1=cw_t[:, jj * 3 + 2:jj * 3 + 3], scalar2=None, op0=ALU.mult),
                             [r_uext, cw_r], [r_fA])
                        P.op("dve", lambda e, jj=jj, cw_t=cw_t: e.scalar_tensor_tensor(out=fA[:, jj, :], in0=uext[:, jj, 1:T + 1], scalar=cw_t[:, jj * 3 + 1:jj * 3 + 2], in1=fA[:, jj, :],
                                                                            op0=ALU.mult, op1=ALU.add), [r_uext, cw_r, r_fA], [r_fA])
                        P.op("dve", lambda e, jj=jj, cw_t=cw_t: e.scalar_tensor_tensor(out=fA[:, jj, :], in0=uext[:, jj, 0:T], scalar=cw_t[:, jj * 3:jj * 3 + 1], in1=fA[:, jj, :],
                                                                            op0=ALU.mult, op1=ALU.add), [r_uext, cw_r, r_fA], [r_fA])
                        yield
                    w_t, w_r = nxt
                    nxt = load_w(7)
                    for jj in range(4):
                        bk, br = proj_feat(w_t, w_r, jj)
                        P.op("dve", lambda e, bk=bk, jj=jj: e.tensor_tensor(out=fA[:, jj, :], in0=bk[:], in1=fA[:, jj, :], op=ALU.mult), [br, r_fA], [r_fA])
                        yield
                    w_t, w_r = nxt
                    nxt = load_w(8)
                    for jj in range(4):
                        bk, br = proj_feat(w_t, w_r, jj)
                        silu_gate(bk, br, fB[:, jj, :], r_fB, jj)
                        P.op("pool", lambda e, jj=jj: e.tensor_tensor(out=ct[:, jj, :], in0=fA[:, jj, :], in1=fB[:, jj, :], op=ALU.mult), [r_fA, r_fB], [r_ct])
                        yield
                    w8 = nxt
                    w9 = load_w(9)
                    for s in range(4):
                        bk, br = proj_tok(w9[0], w9[1], s)
                        P.op("act", lambda e, bk=bk, s=s: e.activation(out=vr[:, s, :], in_=bk[:], func=ACTF.Copy), [br], [r_vr])
                        yield
                    w_t, w_r = w8
                    for s in range(4):
                        rp_t, rp_r = load_rope(s)
                        bk, br = proj_tok(w_t, w_r, s)
                        if s == 3:
                            nxt = load_w(10)
                        yield
                        P.op("act", lambda e, bk=bk: e.activation(out=qk32[:], in_=bk[:], func=ACTF.Copy), [br], [r_qk32])
                        v = qk32[:].rearrange("p (g d) -> p g d", d=64)
                        dv = rqk[:].rearrange("p (g d) -> p g d", d=64)
                        cs = rp_t[:, 0:256].rearrange("p (g f) -> p g f", f=32)
                        sn = rp_t[:, 256:512].rearrange("p (g f) -> p g f", f=32)
                        a1 = rt1[:].rearrange("p (g f) -> p g f", f=32)
                        a2 = rt2[:].rearrange("p (g f) -> p g f", f=32)
                        P.op("dve", lambda e, v=v, cs=cs, a1=a1: e.tensor_tensor(out=a1, in0=v[:, :, 0:32], in1=cs, op=ALU.mult), [r_qk32, rp_r], [r_rt1])
                        P.op("dve", lambda e, v=v, sn=sn, a2=a2: e.tensor_tensor(out=a2, in0=v[:, :, 32:64], in1=sn, op=ALU.mult), [r_qk32, rp_r], [r_rt2])
                        P.op("dve", lambda e, dv=dv, a1=a1, a2=a2: e.tensor_tensor(out=dv[:, :, 0:32], in0=a1, in1=a2, op=ALU.subtract), [r_rt1, r_rt2], [r_rqk])
                        P.op("dve", lambda e, v=v, cs=cs, a1=a1: e.tensor_tensor(out=a1, in0=v[:, :, 32:64], in1=cs, op=ALU.mult), [r_qk32, rp_r, r_rt1], [r_rt1])
                        P.op("dve", lambda e, v=v, sn=sn, a2=a2: e.tensor_tensor(out=a2, in0=v[:, :, 0:32], in1=sn, op=ALU.mult), [r_qk32, rp_r, r_rt2], [r_rt2])
                        P.op("dve", lambda e, dv=dv, a1=a1, a2=a2: e.tensor_tensor(out=dv[:, :, 32:64], in0=a1, in1=a2, op=ALU.add), [r_rt1, r_rt2, r_rqk], [r_rqk])
                        P.op("pool", lambda e: e.tensor_copy(out=rb4[:, 0, :], in_=rqk[:, 0:256]), [r_rqk], [r_rb4])
                        P.op("pool", lambda e: e.tensor_copy(out=rb4[:, 1, :], in_=rqk[:, 256:512]), [r_rqk, r_rb4], [r_rb4])
                        P.op("pool", lambda e: e.tensor_tensor(out=rb4[:, 2, :], in0=rqk[:, 0:256], in1=xiQ[:], op=ALU.mult), [r_rqk, r_xiQ, r_rb4], [r_rb4])
                        P.op("pool", lambda e: e.tensor_tensor(out=rb4[:, 3, :], in0=rqk[:, 256:512], in1=zetaK[:], op=ALU.mult), [r_rqk, r_zetaK, r_rb4], [r_rb4])
                        yield
                        yield
                        for k in range(3):
                            bk2, br2 = nb()
                            for h in range(4):
                                P.op("pe", lambda e, bk2=bk2, h=h, k=k: e.matmul(bk2[0:64, h * 128:(h + 1) * 128], lhsT=rb4[:, k, h * 64:(h + 1) * 64], rhs=identb[:],
                                                                               start=True, stop=True), [r_rb4, r_identb], [br2])
                            P.op("act" if k != 1 else "dve",
                                 (lambda e, bk2=bk2, k=k: e.activation(out=rtr[:, k, :], in_=bk2[0:64, :], func=ACTF.Copy)) if k != 1 else
                                 (lambda e, bk2=bk2, k=k: e.tensor_copy(out=rtr[:, k, :], in_=bk2[0:64, :])), [br2], [r_rtr])
                        yield
                        yield
                        bs, brs = nb()
                        for h in range(4):
                            P.op("pe", lambda e, bs=bs, h=h: e.matmul(bs[:, h * 128:(h + 1) * 128], lhsT=rtr[:, 1, h * 128:(h + 1) * 128], rhs=rtr[:, 0, h * 128:(h + 1) * 128],
                                                                      start=True, stop=True), [r_rtr], [brs])
                        P.op("dve", lambda e, bs=bs: e.tensor_tensor(out=smT[:], in0=bs[:], in1=dmT[:], op=ALU.mult), [brs, r_dmT], [r_smT])
                        bkv, brkv = nb()
                        for h in range(4):
                            P.op("pe", lambda e, bkv=bkv, h=h, s=s: e.matmul(bkv[0:64, h * 128:(h + 1) * 128], lhsT=rb4[:, 3, h * 64:(h + 1) * 64], rhs=vr[:, s, h * 128:(h + 1) * 128],
                                                                             start=True, stop=True), [r_rb4, r_vr], [brkv])
                        yield
                        yield
                        bo, bro = nb()
                        for h in range(4):
                            P.op("pe", lambda e, bo=bo, h=h, s=s: e.matmul(bo[:, h * 128:(h + 1) * 128], lhsT=vr[:, s, h * 128:(h + 1) * 128], rhs=smT[:, h * 128:(h + 1) * 128],
                                                                           start=True, stop=False), [r_vr, r_smT], [bro])
                            P.op("pe", lambda e, bo=bo, h=h: e.matmul(bo[:, h * 128:(h + 1) * 128], lhsT=stateb[:, h * 128:(h + 1) * 128], rhs=rtr[:, 2, h * 128:(h + 1) * 128],
                                                                      start=False, stop=True), [r_stateb, r_rtr], [bro])
                        P.op("act", lambda e, bo=bo, s=s: e.activation(out=fA[:, :, s * 128:(s + 1) * 128], in_=bo[:].rearrange("p (h n) -> p h n", n=128), func=ACTF.Copy),
                             [bro], [r_fA])
                        for h in range(4):
                            P.op("dve", lambda e, bkv=bkv, h=h: e.scalar_tensor_tensor(out=state[:, h * 128:(h + 1) * 128], in0=state[:, h * 128:(h + 1) * 128], scalar=DECAY[h],
                                                                                       in1=bkv[0:64, h * 128:(h + 1) * 128], op0=ALU.mult, op1=ALU.add), [brkv, r_state], [r_state])
                        P.op("dve", lambda e: e.tensor_copy(out=stateb[:], in_=state[:]), [r_state], [r_stateb])
                        yield
                    w_t, w_r = nxt
                    for jj in range(4):
                        bk, br = proj_feat(w_t, w_r, jj)
                        silu_gate(bk, br, fB[:, jj, :], r_fB, jj)
                        yield
                    for h in range(4):
                        sq_t, sq_r = sq[h % 2]
                        P.op("pool", lambda e, sq_t=sq_t, h=h: e.tensor_tensor(out=sq_t[:], in0=fA[:, h, :], in1=fA[:, h, :], op=ALU.mult), [r_fA], [sq_r])
                        bk, br = nb()
                        P.op("pe", lambda e, bk=bk, sq_t=sq_t: e.matmul(bk[:], lhsT=onesf[:], rhs=sq_t[:], start=True, stop=True), [r_onesf, sq_r], [br])
                        rms_scale(bk, 1.0 / 128, s1, r_s1, br)
                        P.op("dve", lambda e, h=h: e.tensor_tensor(out=s4[:], in0=fA[:, h, :], in1=s1[:], op=ALU.mult), [r_fA, r_s1], [r_s4])
                        P.op("pool", lambda e, h=h: e.tensor_tensor(out=rT[:, h, :], in0=s4[:], in1=fB[:, h, :], op=ALU.mult), [r_s4, r_fB], [r_rT])
                        yield

                Sb = [banks[0], banks[1], banks[2]]
                OL = [(banks[3], banks[4])]
                poolB_t = [5, 6, 7]
                acc_t = [(s2, r_s2), (s3, r_s3)]

                def gen_A(t=t, l=l, lt=lt, sg_t=sg_t, lam_init=lam_init):
                    tiles = [(h, c, j) for h in range(4) for c in range(2) for j in range(t + 1)]
                    blocks = [(ti, i) for ti in range(len(tiles)) for i in range(4)]
                    kv = {}

                    def load(ti):
                        h, c, j = tiles[ti]
                        kt_t, kt_r = ktb[ti % 3]
                        vt_t, vt_r = vtb[ti % 3]
                        ksl = slice(j * T, (j + 1) * T)
                        P.dma(lambda e: e.dma_start(out=kt_t[:], in_=ktc_d[l][:, h, c, ksl]), kt_r, reads=[r_kc[l][j]], writes=[kt_r])
                        P.dma(lambda e: e.dma_start(out=vt_t[:], in_=vc_d[l][ksl, h * 128:(h + 1) * 128].rearrange("(i p) n -> p i n", p=128)),
                              vt_r, reads=[r_vc[l][j]], writes=[vt_r])
                        kv[ti] = (kt_t, kt_r, vt_t, vt_r)

                    def qk(b):
                        ti, i = blocks[b]
                        h, c, j = tiles[ti]
                        kt_t, kt_r, vt_t, vt_r = kv[ti]
                        q0 = 128 * i if j == t else 0
                        sbk, sbr = Sb[b % 3]
                        pt_t, pt_r = ptb[b % 4]
                        P.op("pe", lambda e: e.matmul(sbk[:, q0:T], lhsT=kt_t[:, i * 128:(i + 1) * 128], rhs=qt[:, h, q0:T], start=True, stop=True), [kt_r, r_qt], [sbr])
                        P.op("act", lambda e: e.activation(out=pt_t[:, q0:T], in_=sbk[:, q0:T], func=ACTF.Exp), [sbr], [pt_r])
                        if j == t:
                            P.op("pool", lambda e: e.tensor_tensor(out=pt_t[:, q0:q0 + 128], in0=pt_t[:, q0:q0 + 128], in1=mask01[:, 0:128], op=ALU.mult), [pt_r, r_mask01], [pt_r])

                    def pv(b):
                        ti, i = blocks[b]
                        h, c, j = tiles[ti]
                        kt_t, kt_r, vt_t, vt_r = kv[ti]
                        q0 = 128 * i if j == t else 0
                        pt_t, pt_r = ptb[b % 4]
                        (Ob, Obr), (Lb, Lbr) = OL[(h * 2 + c) % len(OL)]
                        first = (j == 0 and i == 0)
                        lastb = (j == t and i == 3)
                        P.op("pe", lambda e: e.matmul(Ob[:, q0:T], lhsT=vt_t[:, i, :], rhs=pt_t[:, q0:T], start=first, stop=lastb), [vt_r, pt_r], [Obr])
                        P.op("pe", lambda e: e.matmul(Lb[:, q0:T], lhsT=onesb[:], rhs=pt_t[:, q0:T], start=first, stop=lastb), [r_onesb, pt_r], [Lbr])
                        if lastb:
                            a_t, a_r = acc_t[c]
                            P.op("dve", lambda e: e.reciprocal(out=s1[:], in_=Lb[:]), [Lbr, r_s1], [r_s1])
                            P.op("dve", lambda e: e.tensor_tensor(out=a_t[:], in0=Ob[:], in1=s1[:], op=ALU.mult), [Obr, r_s1], [a_r])
                            if c == 1:
                                P.op("dve", lambda e: e.scalar_tensor_tensor(out=s2[:], in0=s3[:], scalar=lt[:, 0:1], in1=s2[:], op0=ALU.mult, op1=ALU.add), [r_s3, r_s2, lr], [r_s2])
                                P.op("pool", lambda e: e.tensor_tensor(out=s3[:], in0=s2[:], in1=s2[:], op=ALU.mult), [r_s2], [r_s3])
                                def part2(h=h):
                                    bk, br = nb()
                                    P.op("pe", lambda e: e.matmul(bk[:], lhsT=onesf[:], rhs=s3[:], start=True, stop=True), [r_onesf, r_s3], [br])
                                    rms_scale(bk, 1.0 / 128, s1, r_s1, br)
                                    P.op("dve", lambda e: e.scalar_tensor_tensor(out=s2[:], in0=s2[:], scalar=sg_t[:, 0:1], in1=s1[:], op0=ALU.mult, op1=ALU.mult), [r_s2, sg_r, r_s1], [r_s2])
                                    P.op("dve", lambda e: e.scalar_tensor_tensor(out=aT[:, h, :], in0=s2[:], scalar=1.0 - lam_init, in1=za[:, h, :], op0=ALU.mult, op1=ALU.mult),
                                         [r_s2, r_za], [r_aT])
                                deferred.append([min(6, 4 * (t + 1) - 1), part2])

                    deferred = []
                    load(0)
                    if len(tiles) > 1:
                        load(1)
                    nblk = len(blocks)
                    qk(0)
                    if nblk > 1:
                        qk(1)
                    for b in range(nblk):
                        if b + 2 < nblk:
                            tn, inn = blocks[b + 2]
                            if inn == 0 and tn + 1 < len(tiles):
                                load(tn + 1)
                            qk(b + 2)
                        for d in list(deferred):
                            d[0] -= 1
                            if d[0] <= 0:
                                deferred.remove(d)
                                d[1]()
                        pv(b)
                        yield
                    for d in deferred:
                        d[1]()

                bank_pool[0] = poolB_t
                nA = 8 * 4 * (t + 1)
                nB = 4 * 5 + 4 + 4 * 8 + 4 + 4
                interleave(gen_A(), gen_B(), nA, nB)
                bank_pool[0] = pool_all

                br_src = [(aT, r_aT), (ct, r_ct), (rT, r_rT)]
                for j in range(8):
                    wp_t, wp_r = wpb[j % 2]
                    P.dma(lambda e, wp_t=wp_t, j=j, l=l: e.dma_start(out=wp_t[:], in_=wpostb_d[l][j]), wp_r, reads=[wres[f"post{l}_{j}"]], writes=[wp_r])
                    gv = wp_t[:, 0:3072].rearrange("p (c i n) -> p c i n", i=3, n=128)
                    bv = wp_t[:, 3072:4608].rearrange("p (i k n) -> p i k n", k=4, n=128)
                    for i in range(3):
                        gb, gbr = nb()
                        for c in range(NCH):
                            P.op("pe", lambda e, gb=gb, c=c, i=i, gv=gv: e.matmul(gb[:], lhsT=gv[:, c, i, :], rhs=ht[:, c, :], start=(c == 0), stop=(c == NCH - 1)), [wp_r, r_ht], [gbr])
                        g_t, g_r = gtb[i]
                        P.op("act", lambda e, gb=gb, g_t=g_t: e.activation(out=g_t[:], in_=gb[:], func=ACTF.Sigmoid), [gbr], [g_r])
                        pb, pbr = nb()
                        s_t, s_r = br_src[i]
                        for k in range(4):
                            P.op("pe", lambda e, pb=pb, k=k, i=i, bv=bv, s_t=s_t: e.matmul(pb[:], lhsT=bv[:, i, k, :], rhs=s_t[:, k, :], start=(k == 0), stop=(k == 3)), [wp_r, s_r], [pbr])
                        if i == 0:
                            P.op("dve", lambda e, pb=pb, g_t=g_t: e.tensor_tensor(out=s4[:], in0=pb[:], in1=g_t[:], op=ALU.mult), [pbr, g_r], [r_s4])
                        else:
                            P.op("dve", lambda e, pb=pb, g_t=g_t: e.tensor_tensor(out=g_t[:], in0=pb[:], in1=g_t[:], op=ALU.mult), [pbr, g_r], [g_r])
                            if i == 1:
                                P.op("pool", lambda e, g_t=g_t: e.tensor_tensor(out=s4[:], in0=s4[:], in1=g_t[:], op=ALU.add), [r_s4, g_r], [r_s4])
                            else:
                                P.op("pool", lambda e, g_t=g_t, j=j: e.tensor_tensor(out=mg[:, j, :], in0=s4[:], in1=g_t[:], op=ALU.add), [r_s4, g_r], [r_mg])
                nxt_tile = (l, t + 1) if t + 1 < NT else ((l + 1, 0) if l + 1 < NL else None)
                if nxt_tile is not None:
                    emit_norm(*nxt_tile)
                for jo in range(8):
                    wo_t, wo_r = wob[jo % 2]
                    P.dma(lambda e, wo_t=wo_t, jo=jo, l=l: e.dma_start(out=wo_t[:].rearrange("p c n -> p (c n)"), in_=woutb_d[l][jo]), wo_r, reads=[wres[f"out{l}_{jo}"]], writes=[wo_r])
                    yb, ybr = nb()
                    for c in range(NCH):
                        P.op("pe", lambda e, yb=yb, c=c, wo_t=wo_t: e.matmul(yb[:], lhsT=wo_t[:, c, :], rhs=mg[:, c, :], start=(c == 0), stop=(c == NCH - 1)), [wo_r, r_mg], [ybr])
                    P.op("dve", lambda e, xc=xc, yb=yb, jo=jo: e.tensor_tensor(out=xc[:, jo, :], in0=yb[:], in1=xc[:, jo, :], op=ALU.add), [ybr, r_xc], [r_xc])
                if DBG and l == 0 and t == DBG - 1:
                    dbgs, r_dbgs = fA[:].rearrange("p a b -> p (a b)"), r_fA
                    for k, (tt, rr) in enumerate([(aT, r_aT), (ct, r_ct), (rT, r_rT), (mg, r_mg), (qt, r_qt)]):
                        P.op("dve", lambda e, tt=tt: e.tensor_copy(out=dbgs, in_=tt[:, 0:4, :].rearrange("p a b -> p (a b)")), [rr], [r_dbgs])
                        final.append(P.dma(lambda e, k=k: e.dma_start(out=dbg_d[:, k, :], in_=dbgs), r_dbgs, reads=[r_dbgs]))
                if not last:
                    P.dma(lambda e, xc=xc, tsl=tsl: e.dma_start(out=x1T_d.rearrange("(c p) t -> p c t", p=128)[:, :, tsl], in_=xc[:]), r_xc, reads=[r_xc], writes=[r_x1[t]], eng=STQ)
                else:
                    bk, br = nb()
                    for c in range(NCH):
                        sq_t, sq_r = sq[c % 2]
                        P.op("pool", lambda e, xc=xc, sq_t=sq_t, c=c: e.tensor_tensor(out=sq_t[:], in0=xc[:, c, :], in1=xc[:, c, :], op=ALU.mult), [r_xc], [sq_r])
                        P.op("pe", lambda e, sq_t=sq_t, c=c, bk=bk: e.matmul(bk[:], lhsT=onesf[:], rhs=sq_t[:], start=(c == 0), stop=(c == NCH - 1)), [r_onesf, sq_r], [br])
                    rms_scale(bk, 1.0 / D, rstd, r_rstd, br)
                    for c in range(NCH):
                        P.op("dve", lambda e, xc=xc, c=c: e.scalar_tensor_tensor(out=xc[:, c, :], in0=xc[:, c, :], scalar=fng[:, c:c + 1], in1=rstd[:], op0=ALU.mult, op1=ALU.mult),
                             [r_xc, r_fng, r_rstd], [r_xc])
                    final.append(P.dma(lambda e, xc=xc, tsl=tsl: e.dma_start(out=outT_d.rearrange("(c p) t -> p c t", p=128)[:, :, tsl], in_=xc[:]), r_xc, reads=[r_xc], eng=STQ))

        P.emit(st, final_waits=final)
        build.stats = (P.stats, P.nwaits)
    return nc


_CACHE = {}


def _prep_inputs(x, norm_g, w_in, attn_lambda, attn_subln_g, conv_w, w_branch, w_out, final_norm_g):
    B, S, _ = x.shape
    NL = w_in.shape[0]
    tabs, _ = _const_tables(S)
    common = dict(tabs)
    common["fng"] = np.ascontiguousarray(np.asarray(final_norm_g, np.float32).reshape(NCH, 128).T)
    for l in range(NL):
        win, wpost, wo = _layout_weights(np.asarray(w_in[l], np.float32), np.asarray(w_branch[l], np.float32), np.asarray(w_out[l], np.float32))
        common[f"win{l}"] = win
        common[f"wpost{l}"] = wpost
        common[f"wout{l}"] = wo
        common[f"ng{l}"] = np.ascontiguousarray(np.asarray(norm_g[l], np.float32).reshape(NCH, 128).T)
        common[f"sg{l}"] = np.ascontiguousarray(np.asarray(attn_subln_g[l], np.float32).reshape(128, 1))
        cwl = np.asarray(conv_w[l], np.float32)
        common[f"cw{l}"] = np.ascontiguousarray(cwl.reshape(3, 4, 128).transpose(2, 1, 0)).reshape(128, 12)
        common[f"al{l}"] = np.ascontiguousarray(np.broadcast_to(np.asarray(attn_lambda[l], np.float32).reshape(1, 256), (128, 256)))
    in_maps = []
    for b in range(B):
        m = dict(common)
        m["xT"] = np.ascontiguousarray(np.asarray(x[b], np.float32).T)
        in_maps.append(m)
    return in_maps, B, S, NL


def kernel(x, norm_g, w_in, attn_lambda, attn_subln_g, conv_w, w_branch, w_out, final_norm_g):
    in_maps, B, S, NL = _prep_inputs(x, norm_g, w_in, attn_lambda, attn_subln_g, conv_w, w_branch, w_out, final_norm_g)
    nc = build(S, NL)
    res = run_bass_kernel_spmd(nc, in_maps, core_ids=list(range(B)))
    if DBG:
        kernel.dbg = res.results[0]["dbg"]
    out = np.stack([np.ascontiguousarray(res.results[b]["outT"].T) for b in range(B)], axis=0)
    return out.astype(np.float32)
```
